# Optimizing a Trainium2 kernel written in Bass

```python
import math
import jax, jax.numpy as jnp
from jax import lax
import numpy as np

D_MODEL = 2048
BATCH = 4
SEQ = 2048
DEPTH = 1
DEC_BATCH = 128
DEC_SEQ = 8
PAST_LEN = 16384
PAGE_SIZE = 128

N_MEM = 256
D_CONV = 1024
CONV_WIDTH = 31
CONV_TAIL = CONV_WIDTH - 1
HGRN_HEADS = 8
HGRN_DK = 128
HGRN_DV = 128
D_HGRN = HGRN_HEADS * HGRN_DV
D_HGRN_F = HGRN_HEADS * HGRN_DK
ATTN_HEADS = 4
ATTN_HEAD_DIM = 256
D_ATTN = ATTN_HEADS * ATTN_HEAD_DIM
N_BRANCH = 3
HGRN_CHUNK = 64
EPS = 1e-6
SPLITS = (D_CONV, D_CONV, D_CONV, D_HGRN_F, D_HGRN_F, D_HGRN, D_HGRN, D_ATTN, D_ATTN, N_BRANCH * D_MODEL)
D_IN = sum(SPLITS)

kernel_name = "hybrid_conformer_hgrn2_memattn_step"


def rmsnorm(x, g):
    xf = x.astype(jnp.float32)
    y = xf * lax.rsqrt(jnp.mean(xf * xf, axis=-1, keepdims=True) + EPS)
    return (y * g.astype(jnp.float32)).astype(x.dtype)


def layernorm(x, g, b):
    xf = x.astype(jnp.float32)
    mu = jnp.mean(xf, axis=-1, keepdims=True)
    var = jnp.mean(jnp.square(xf - mu), axis=-1, keepdims=True)
    y = (xf - mu) * lax.rsqrt(var + EPS)
    return (y * g.astype(jnp.float32) + b.astype(jnp.float32)).astype(x.dtype)


def causal_dwconv(u, tail, w, b):
    up = jnp.concatenate([tail, u], axis=1)
    y = lax.conv_general_dilated(up, w[:, None, :], window_strides=(1,), padding='VALID',
                                 dimension_numbers=('NWC', 'WIO', 'NWC'), feature_group_count=D_CONV)
    return y + b, up[:, -CONV_TAIL:]


def hgrn2_scan(q, f, v, S0):
    N, L = q.shape[0], q.shape[1]
    C = math.gcd(L, HGRN_CHUNK)
    nc = L // C
    g = jnp.log(f)
    k = 1.0 - f

    def to_chunks(t):
        return t.reshape(N, nc, C, HGRN_HEADS, t.shape[-1]).transpose(1, 0, 3, 2, 4)

    mask = jnp.tril(jnp.ones((C, C), dtype=bool))

    def step(S, xs):
        qc, kc, gc, vc = xs
        b = jnp.cumsum(gc, axis=2)
        diff = b[:, :, :, None, :] - b[:, :, None, :, :]
        decay = jnp.exp(jnp.where(mask[None, None, :, :, None], diff, -jnp.inf))
        A = jnp.einsum('nhtk,nhtsk,nhsk->nhts', qc, decay, kc)
        o = jnp.einsum('nhts,nhsv->nhtv', A, vc) + jnp.einsum('nhtk,nhkv->nhtv', qc * jnp.exp(b), S)
        bC = b[:, :, -1]
        S_new = jnp.exp(bC)[..., None] * S + jnp.einsum('nhsk,nhsv->nhkv', kc * jnp.exp(bC[:, :, None, :] - b), vc)
        return S_new, o

    S_fin, o = lax.scan(step, S0.astype(jnp.float32), (to_chunks(q), to_chunks(k), to_chunks(g), to_chunks(v)))
    o = o.transpose(1, 0, 3, 2, 4).reshape(N, L, HGRN_HEADS, HGRN_DV)
    return o, S_fin


def mem_kv(mem, g_mem, w_mem_kv):
    m = rmsnorm(mem, g_mem) @ w_mem_kv
    mk, mv = jnp.split(m, 2, axis=-1)
    N = mem.shape[0]
    return (mk.reshape(N, N_MEM, ATTN_HEADS, ATTN_HEAD_DIM), mv.reshape(N, N_MEM, ATTN_HEADS, ATTN_HEAD_DIM))


def mixer_layer(h, conv_tail, S0, mk, mv, lb, w_in, conv_w, conv_b, ln_conv_g, ln_conv_b, w_conv_out,
                g_hgrn_norm, w_hgrn_out, w_attn_out, b_gate, w_out):
    N, L = h.shape[0], h.shape[1]
    proj = h @ w_in
    idx = [int(s) for s in np.cumsum(SPLITS)[:-1]]
    c_val, c_glu, c_silu, hq, hf, hi, hog, aq, a_silu, gate_logits = jnp.split(proj, idx, axis=-1)

    u = c_val * jax.nn.sigmoid(c_glu)
    dc, new_tail = causal_dwconv(u, conv_tail, conv_w, conv_b)
    a = jax.nn.silu(layernorm(dc, ln_conv_g, ln_conv_b)) * jax.nn.silu(c_silu)
    pA = a @ w_conv_out

    f = lb + (1.0 - lb) * jax.nn.sigmoid(hf.astype(jnp.float32))
    q = hq.astype(jnp.float32).reshape(N, L, HGRN_HEADS, HGRN_DK) * (HGRN_DK ** -0.5)
    f = f.reshape(N, L, HGRN_HEADS, HGRN_DK)
    v = hi.astype(jnp.float32).reshape(N, L, HGRN_HEADS, HGRN_DV)
    o, S_new = hgrn2_scan(q, f, v, S0)
    o = rmsnorm(o, g_hgrn_norm.reshape(HGRN_HEADS, HGRN_DV)).reshape(N, L, D_HGRN).astype(h.dtype)
    pB = (o * jax.nn.silu(hog)) @ w_hgrn_out

    aqh = aq.reshape(N, L, ATTN_HEADS, ATTN_HEAD_DIM).astype(jnp.float32)
    s = jnp.einsum('nlhd,nmhd->nhlm', aqh, mk.astype(jnp.float32)) * (ATTN_HEAD_DIM ** -0.5)
    pr = jax.nn.softmax(s, axis=-1)
    ao = jnp.einsum('nhlm,nmhd->nlhd', pr, mv.astype(jnp.float32)).reshape(N, L, D_ATTN).astype(h.dtype)
    pC = (ao * jax.nn.silu(a_silu)) @ w_attn_out

    gt = jax.nn.sigmoid(gate_logits.reshape(N, L, N_BRANCH, D_MODEL) + b_gate)
    merged = gt[:, :, 0] * pA + gt[:, :, 1] * pB + gt[:, :, 2] * pC
    return merged @ w_out, new_tail, S_new.astype(S0.dtype)


def setup_inputs(seed: int = 0) -> dict:
    key = jax.random.key(seed)
    ks = jax.random.split(key, 24)
    nrm = jax.random.normal
    f32 = jnp.float32
    return {
        "x_prompt": nrm(ks[0], (BATCH, SEQ, D_MODEL), f32),
        "x_sample": nrm(ks[1], (DEC_BATCH, DEC_SEQ, D_MODEL), f32),
        "state_conv": 0.5 * nrm(ks[2], (DEPTH, DEC_BATCH, CONV_TAIL, D_CONV), f32),
        "state_hgrn": 0.5 * nrm(ks[3], (DEPTH, DEC_BATCH, HGRN_HEADS, HGRN_DK, HGRN_DV), f32),
        "cache_mem_k": nrm(ks[4], (DEPTH, DEC_BATCH, N_MEM, ATTN_HEADS, ATTN_HEAD_DIM), f32),
        "cache_mem_v": nrm(ks[5], (DEPTH, DEC_BATCH, N_MEM, ATTN_HEADS, ATTN_HEAD_DIM), f32),
        "mem_prompt": nrm(ks[6], (BATCH, N_MEM, D_MODEL), f32),
        "g_pre": 1.0 + 0.02 * nrm(ks[7], (DEPTH, D_MODEL), f32),
        "w_in": nrm(ks[8], (DEPTH, D_MODEL, D_IN), f32) * D_MODEL ** -0.5,
        "conv_w": nrm(ks[9], (DEPTH, CONV_WIDTH, D_CONV), f32) * CONV_WIDTH ** -0.5,
        "conv_b": 0.01 * nrm(ks[10], (DEPTH, D_CONV), f32),
        "ln_conv_g": 1.0 + 0.02 * nrm(ks[11], (DEPTH, D_CONV), f32),
        "ln_conv_b": 0.01 * nrm(ks[12], (DEPTH, D_CONV), f32),
        "w_conv_out": nrm(ks[13], (DEPTH, D_CONV, D_MODEL), f32) * D_CONV ** -0.5,
        "lb_logits": 0.1 * nrm(ks[14], (DEPTH + 1, D_HGRN_F), f32),
        "g_hgrn_norm": 1.0 + 0.02 * nrm(ks[15], (DEPTH, D_HGRN), f32),
        "w_hgrn_out": nrm(ks[16], (DEPTH, D_HGRN, D_MODEL), f32) * D_HGRN ** -0.5,
        "g_mem": 1.0 + 0.02 * nrm(ks[17], (DEPTH, D_MODEL), f32),
        "w_mem_kv": nrm(ks[18], (DEPTH, D_MODEL, 2 * D_ATTN), f32) * D_MODEL ** -0.5,
        "w_attn_out": nrm(ks[19], (DEPTH, D_ATTN, D_MODEL), f32) * D_ATTN ** -0.5,
        "b_gate": 0.01 * nrm(ks[20], (DEPTH, N_BRANCH, D_MODEL), f32),
        "w_out": nrm(ks[21], (DEPTH, D_MODEL, D_MODEL), f32) * D_MODEL ** -0.5,
        "g_final": 1.0 + 0.02 * nrm(ks[22], (D_MODEL,), f32),
    }


def reference(x_prompt, x_sample, state_conv, state_hgrn, cache_mem_k, cache_mem_v, mem_prompt,
              g_pre, w_in, conv_w, conv_b, ln_conv_g, ln_conv_b, w_conv_out, lb_logits, g_hgrn_norm,
              w_hgrn_out, g_mem, w_mem_kv, w_attn_out, b_gate, w_out, g_final):
    lb_all = jnp.cumsum(jax.nn.softmax(lb_logits.astype(jnp.float32), axis=0), axis=0)
    hp, hs = x_prompt, x_sample
    tails_p, S_p, mk_p, mv_p, tails_s, S_s = [], [], [], [], [], []
    for l in range(DEPTH):
        layer_w = (w_in[l], conv_w[l], conv_b[l], ln_conv_g[l], ln_conv_b[l], w_conv_out[l],
                   g_hgrn_norm[l], w_hgrn_out[l], w_attn_out[l], b_gate[l], w_out[l])
        mk, mv = mem_kv(mem_prompt, g_mem[l], w_mem_kv[l])
        tail0 = jnp.zeros((hp.shape[0], CONV_TAIL, D_CONV), hp.dtype)
        S0 = jnp.zeros((hp.shape[0], HGRN_HEADS, HGRN_DK, HGRN_DV), hp.dtype)
        yp, tp, sp = mixer_layer(rmsnorm(hp, g_pre[l]), tail0, S0, mk, mv, lb_all[l], *layer_w)
        hp = hp + yp
        ys, ts, ss = mixer_layer(rmsnorm(hs, g_pre[l]), state_conv[l], state_hgrn[l],
                                 cache_mem_k[l], cache_mem_v[l], lb_all[l], *layer_w)
        hs = hs + ys
        tails_p.append(tp); S_p.append(sp); mk_p.append(mk); mv_p.append(mv)
        tails_s.append(ts); S_s.append(ss)
    y_prompt = rmsnorm(hp, g_final)
    y_sample = rmsnorm(hs, g_final)
    return (y_prompt, y_sample, jnp.stack(tails_p), jnp.stack(S_p), jnp.stack(mk_p), jnp.stack(mv_p),
            jnp.stack(tails_s), jnp.stack(S_s))
```

```python
import numpy as np
import contextlib
import concourse.bass as bass
import concourse.mybir as mybir
from concourse.bass_utils import run_bass_kernel_spmd

F32 = mybir.dt.float32
BF16 = mybir.dt.bfloat16
AF = mybir.ActivationFunctionType
ALU = mybir.AluOpType
AX = mybir.AxisListType

P = 128
D = 2048
DIN = 15360
TP = 1024
TS = 128
T = TP + TS
TAIL = 32
TT = TAIL + T
NSEQ = 16
LS = 8
EPS = 1e-6
NCORES = 8
KC = D // P

C_GPRE, C_GMEM, C_CONVB, C_LNG, C_LNB, C_LB0, C_LB1, C_GHG, C_BGATE, C_CONVW = 0, 16, 32, 40, 48, 56, 64, 72, 80, 128
NCOLS = 128 + 31 * 8
K_IDENT, K_MPR, K_MSM, K_SEL, K_RST = 0, 128, 256, 384, 400
NCONST = 400 + T


class Tok:
    __slots__ = ("sem", "val", "key")

    def __init__(self, sem, val, key):
        self.sem, self.val, self.key = sem, val, key


_EPOCH = [0]
_SNAP = {}


class Buf:
    __slots__ = ("name", "w", "r", "epoch", "fenced")

    def __init__(self, name=""):
        self.name = name
        self.w = None
        self.r = []
        self.epoch = _EPOCH[0]
        self.fenced = False


class DSem:
    def __init__(self, h, key):
        self.h, self.key, self.count = h, key, 0


class EngRec:
    def __init__(self, name, sem, key):
        self.name, self.sem, self.key = name, sem, key
        self.count = 0
        self.ops = []
        self.waited = {}


class KB:
    def __init__(self, nc):
        _EPOCH[0] = 0
        _SNAP.clear()
        self.nc = nc
        self.nsem = 0
        self.E = {}
        for n in ("pe", "act", "dve", "pool", "sp"):
            self.E[n] = EngRec(n, self.new_sem("e_" + n), n)
        self.dsems = []

    def new_sem(self, name):
        self.nsem += 1
        return self.nc.alloc_semaphore(name=name + str(self.nsem))

    def dsem(self):
        d = DSem(self.new_sem("d"), "d%d" % len(self.dsems))
        self.dsems.append(d)
        return d

    def _waits(self, e, reads, writes, extra=()):
        E = self.E[e]
        deps = []
        for b in reads:
            if b.w is not None:
                deps.append(b.w)
        for b in writes:
            if b.w is not None:
                deps.append(b.w)
            for t in b.r:
                deps.append(t)
        deps.extend(extra)
        for b in list(reads) + list(writes):
            if not b.fenced:
                b.fenced = True
                deps.extend(_SNAP.get(b.epoch, ()))
        if e == "pe":
            deps = [t for t in deps if t.key != "pe"]
        ws = []
        for t in deps:
            if E.waited.get(t.key, 0) < t.val:
                E.waited[t.key] = t.val
                ws.append((t.sem, t.val))
        best = {}
        for s, v in ws:
            k = id(s)
            if k not in best or best[k][1] < v:
                best[k] = (s, v)
        return list(best.values())

    def _commit(self, tok, reads, writes):
        for b in reads:
            b.r.append(tok)
        for b in writes:
            b.w = tok
            b.r = []

    def op(self, e, fn, reads, writes, signal=True, **kw):
        E = self.E[e]
        ws = self._waits(e, reads, writes)
        if signal:
            E.count += 1
            tok = Tok(E.sem, E.count, e)
            E.ops.append((ws, fn, kw, (E.sem, 1)))
            self._commit(tok, reads, writes)
            return tok
        E.ops.append((ws, fn, kw, None))
        return None

    def dma(self, q, ds, out, in_, reads, writes, **kw):
        E = self.E[q]
        ws = self._waits(q, reads, writes)
        ds.count += 16
        tok = Tok(ds.h, ds.count, ds.key)
        kw = dict(kw)
        kw.update(out=out, in_=in_)
        E.ops.append((ws, "dma_start", kw, (ds.h, 16)))
        self._commit(tok, reads, writes)
        return tok

    def mm_group(self, out, pairs, reads, writes):
        E = self.E["pe"]
        ws = self._waits("pe", reads, writes)
        n = len(pairs)
        for i, pr in enumerate(pairs):
            if len(pr) == 3:
                o, l, r = pr
            else:
                o = out
                l, r = pr
            kw = dict(out=o, lhsT=l, rhs=r, start=(i == 0), stop=(i == n - 1))
            last = i == n - 1
            if last:
                E.count += 1
            E.ops.append((ws if i == 0 else [], "matmul", kw, (E.sem, 1) if last else None))
        tok = Tok(E.sem, E.count, "pe")
        self._commit(tok, reads, writes)
        return tok

    def transposes(self, items, reads, writes):
        E = self.E["pe"]
        ws = self._waits("pe", reads, writes)
        n = len(items)
        for i, (o, a, idn) in enumerate(items):
            last = i == n - 1
            if last:
                E.count += 1
            E.ops.append((ws if i == 0 else [], "transpose", dict(out=o, in_=a, identity=idn), (E.sem, 1) if last else None))
        tok = Tok(E.sem, E.count, "pe")
        self._commit(tok, reads, writes)
        return tok

    def barrier(self):
        toks = [Tok(E.sem, E.count, E.key) for E in self.E.values() if E.count > 0]
        toks += [Tok(d.h, d.count, d.key) for d in self.dsems if d.count > 0]
        for e, E in self.E.items():
            ws = []
            for t in toks:
                if t.key != e and E.waited.get(t.key, 0) < t.val:
                    E.waited[t.key] = t.val
                    ws.append((t.sem, t.val))
            if ws:
                E.ops.append((ws, None, None, None))

    def soft_barrier(self):
        toks = [Tok(E.sem, E.count, E.key) for E in self.E.values() if E.count > 0]
        toks += [Tok(d.h, d.count, d.key) for d in self.dsems if d.count > 0]
        _EPOCH[0] += 1
        _SNAP[_EPOCH[0]] = toks

    def pe_fence(self):
        E = self.E["pe"]
        ws = []
        for e2 in ("act", "dve", "pool"):
            E2 = self.E[e2]
            if E2.count > 0 and E.waited.get(e2, 0) < E2.count:
                E.waited[e2] = E2.count
                ws.append((E2.sem, E2.count))
        if ws:
            E.ops.append((ws, None, None, None))

    def replay(self, eng, e):
        for ws, fn, kw, inc in self.E[e].ops:
            for s, v in ws:
                eng.wait_ge(s, v)
            if fn is None:
                continue
            ins = getattr(eng, fn)(**kw)
            if inc is not None:
                ins.then_inc(inc[0], inc[1])


def build_program(phases=("all",)):
    nc = bass.Bass("TRN2", target_bir_lowering=False)
    kb = KB(nc)
    ALL = "all" in phases
    din = lambda n, s: nc.dram_tensor(n, list(s), F32, kind="ExternalInput").ap()
    dout = lambda n, s: nc.dram_tensor(n, list(s), F32, kind="ExternalOutput").ap()
    x_d = din("x", (T, D))
    xp_d = din("xprev", (TP, D))
    mem_d = din("mem", (256, D))
    kc_d = din("kc", (NSEQ, 256, 1024))
    vc_d = din("vc", (NSEQ, 256, 1024))
    sconv_d = din("sconv", (NSEQ, 30, 1024))
    shg_d = din("shgrn", (NSEQ, 8, P, P))
    w_in_d = din("w_in", (D, DIN))
    w_co_d = din("w_conv_out", (1024, D))
    w_ho_d = din("w_hgrn_out", (1024, D))
    w_ao_d = din("w_attn_out", (1024, D))
    w_kv_d = din("w_mem_kv", (D, D))
    w_out_d = din("w_out", (D, D))
    cols_d = din("cols", (P, NCOLS))
    consts_d = din("consts", (P, NCONST))
    gfin_d = din("gfin", (P, D))
    y_d = dout("y", (T, D))
    convp_d = dout("conv_p", (32, 1024))
    hgp_d = dout("hgrn_p", (8, P, P))
    mk_d = dout("mk", (256, 1024))
    mv_d = dout("mv", (256, 1024))
    convs_d = dout("conv_s", (NSEQ, 30, 1024))
    hgs_d = dout("hgrn_s", (NSEQ, 8, P, P))
    dbg = {}
    if "dbg" in phases:
        dbg["a"] = nc.dram_tensor("dbg_a", [P, 8, T], BF16, kind="ExternalOutput").ap()
        dbg["ob"] = nc.dram_tensor("dbg_ob", [P, 8, T], BF16, kind="ExternalOutput").ap()
        dbg["ao"] = nc.dram_tensor("dbg_ao", [P, 8, T], BF16, kind="ExternalOutput").ap()
        dbg["mg"] = nc.dram_tensor("dbg_mg", [P, 16, T], BF16, kind="ExternalOutput").ap()

    KIB = 1024
    ARENA = 200 * KIB
    ES = contextlib.ExitStack()
    uniq = {"n": 0}

    def ps(st, name, shape, dt):
        uniq["n"] += 1
        kb.pe_fence()
        return st.enter_context(nc.psum_tensor("p%d_%s" % (uniq["n"], name), list(shape), dt))

    with ES:
        arena = ES.enter_context(nc.sbuf_tensor("arena", [P, ARENA // 4], F32))

        class Region:
            def __init__(self, lo, hi):
                self.lo, self.hi, self.p = lo, hi, lo

            def reset(self):
                self.p = self.lo

            def alloc(self, shape, dt):
                es = 2 if dt == BF16 else 4
                n = 1
                for d in shape[1:]:
                    n *= d
                nb = (n * es + 31) // 32 * 32
                off = self.p
                self.p += nb
                assert self.p <= self.hi, ("region overflow", shape, self.p, self.hi)
                ap = arena[0:shape[0], off // 4:(off + nb) // 4]
                if dt == BF16:
                    ap = ap.bitcast(BF16)
                ap = ap[:, 0:n]
                if len(shape) == 3:
                    ap = ap.rearrange("p (a b) -> p a b", a=shape[1])
                elif len(shape) == 4:
                    ap = ap.rearrange("p (a b c) -> p a b c", a=shape[1], b=shape[2])
                elif len(shape) == 5:
                    ap = ap.rearrange("p (a b c d) -> p a b c d", a=shape[1], b=shape[2], c=shape[3])
                return ap

        R_G0 = Region(0, 14 * KIB)
        R_S = Region(14 * KIB, 26 * KIB)
        R_W = Region(26 * KIB, 58 * KIB)
        R_HT = Region(58 * KIB, 95 * KIB)
        R_OB = Region(95 * KIB, 113 * KIB)
        R_A = Region(113 * KIB, 131 * KIB)
        R_AO = Region(131 * KIB, 149 * KIB)

        cols = R_G0.alloc((P, NCOLS), F32)
        consts = R_G0.alloc((P, NCONST), F32)
        ident_bf = R_G0.alloc((P, P), BF16)
        ones_bf = R_G0.alloc((P, P), BF16)
        lbc = R_G0.alloc((P, 32), F32)
        tailT = R_G0.alloc((P, KC, TAIL), BF16)
        gstat = R_G0.alloc((P, 96), F32)
        hT = R_HT.alloc((P, KC, TT), BF16)
        ob = R_OB.alloc((P, 8, T), BF16)
        a_ = R_A.alloc((P, 8, T), BF16)
        ao = R_AO.alloc((P, 8, T), BF16)
        wsm = [R_W.alloc((P, 4096), BF16) for i in range(4)]
        wsmB = [Buf("w%d" % i) for i in range(4)]
        wsmS = [kb.dsem() for i in range(4)]
        wbig = [arena[:, (26 * KIB + i * 16 * KIB) // 4:(26 * KIB + (i + 1) * 16 * KIB) // 4].bitcast(BF16) for i in range(2)]
        wstate = {"p": 0}
        colsB, constsB, identB, onesB, lbcB, hTB, tailB = Buf(), Buf(), Buf(), Buf(), Buf(), Buf(), Buf()
        obB, aB, aoB = Buf(), Buf(), Buf()
        ident_f = consts[:, K_IDENT:K_IDENT + P]
        mask_pr = consts[:, K_MPR:K_MPR + P]
        mask_sm = consts[:, K_MSM:K_MSM + P]

        ds0 = kb.dsem()
        kb.dma("sp", ds0, cols, cols_d[:, :], [], [colsB])
        kb.dma("sp", ds0, consts, consts_d[:, :], [], [constsB])
        kb.op("dve", "tensor_copy", [constsB], [identB], out=ident_bf, in_=ident_f)
        kb.op("dve", "memset", [], [onesB], ap=ones_bf, constant=1.0)
        kb.op("dve", "tensor_tensor", [colsB], [lbcB], out=lbc[:, 0:8], in0=cols[:, C_LB1:C_LB1 + 8], in1=cols[:, C_LB0:C_LB0 + 8], op=ALU.subtract)
        kb.op("act", "activation", [lbcB], [lbcB], out=lbc[:, 8:16], in_=lbc[:, 0:8], func=AF.Sigmoid)
        kb.op("dve", "tensor_scalar", [lbcB], [lbcB], out=lbc[:, 16:24], in0=lbc[:, 8:16], scalar1=-1.0, scalar2=None, op0=ALU.mult)

        wcache = {}

        def wload_big(dram2d, kc, n, key=None):
            if key is not None and key in wcache:
                return wcache.pop(key)
            p_ = wstate["p"]
            if p_ % 2:
                p_ += 1
            i = (p_ // 2) % 2
            wstate["p"] = (p_ + 2) % 4
            view = wbig[i][:, 0:kc * n].rearrange("p (k n) -> p k n", k=kc)
            bs = [wsmB[2 * i], wsmB[2 * i + 1]]
            kb.dma("pool", wsmS[2 * i], view, dram2d.rearrange("(k p) n -> p k n", p=P), [], bs)
            return view, bs

        def wload_sm(dram2d, kc, n, key=None):
            if key is not None and key in wcache:
                return wcache.pop(key)
            i = wstate["p"] % 4
            wstate["p"] = (i + 1) % 4
            view = wsm[i][:, 0:kc * n].rearrange("p (k n) -> p k n", k=kc)
            kb.dma("pool", wsmS[i], view, dram2d.rearrange("(k p) n -> p k n", p=P), [], [wsmB[i]])
            return view, [wsmB[i]]

        def norm_phase(R, st, jobs, tag):
            xt = [R.alloc((P, D), F32) for i in range(2)]
            xtB = [Buf() for _ in range(2)]
            xtS = [kb.dsem() for _ in range(2)]
            junk = R.alloc((P, D), BF16)
            junkB = Buf()
            xs = [R.alloc((P, D), BF16) for i in range(2)]
            xsB = [Buf() for _ in range(2)]
            stat = R.alloc((P, 3 * len(jobs)), F32)
            tp = [ps(st, tag + "tp%d" % i, (P, KC, P), BF16) for i in range(2)]
            tpB = [Buf() for _ in range(2)]
            for j, (rows, gc, dst, dstB) in enumerate(jobs):
                s = j % 2
                kb.dma("sp", xtS[s], xt[s], rows, [], [xtB[s]])
                sB = Buf()
                kb.op("act", "activation", [xtB[s]], [junkB, sB], out=junk, in_=xt[s], func=AF.Square, accum_out=stat[:, 3 * j:3 * j + 1])
                kb.op("act", "activation", [sB], [sB], out=stat[:, 3 * j + 1:3 * j + 2], in_=stat[:, 3 * j:3 * j + 1], func=AF.Ln, scale=1.0 / D, bias=EPS)
                kb.op("act", "activation", [sB], [sB], out=stat[:, 3 * j + 2:3 * j + 3], in_=stat[:, 3 * j + 1:3 * j + 2], func=AF.Exp, scale=-0.5)
                kb.op("dve", "tensor_scalar", [xtB[s], sB], [xsB[s]], out=xs[s], in0=xt[s], scalar1=stat[:, 3 * j + 2:3 * j + 3], scalar2=None, op0=ALU.mult)
                kb.transposes([(tp[s][:, k, :], xs[s][:, k * P:(k + 1) * P], ident_bf) for k in range(KC)], [xsB[s], identB], [tpB[s]])
                kb.op("dve", "tensor_tensor", [tpB[s], colsB], [dstB], out=dst, in0=tp[s][:], in1=cols[:, gc:gc + KC].unsqueeze(2).to_broadcast([P, KC, P]), op=ALU.mult)

        Sf = [R_S.alloc((P, 8, P), F32) for i in range(2)]
        Sb = [R_S.alloc((P, 8, P), BF16) for i in range(2)]
        SfB = [[Buf() for h in range(8)] for i in range(2)]
        SbB = [[Buf() for h in range(8)] for i in range(2)]
        for h in range(8):
            kb.op("pool", "memset", [], [SfB[0][h]], ap=Sf[0][:, h, :], constant=0.0)
            kb.op("pool", "memset", [], [SbB[0][h]], ap=Sb[0][:, h, :], constant=0.0)
        cur_h = [0] * 8

        def hgrn_v(st, hsrc, hsrcB, col0, ntile, vdst, vB, blk, tag):
            with contextlib.ExitStack() as s2:
                pv = [ps(s2, tag + "pv%d" % i, (P, 512), F32) for i in range(2)]
                pvB = [Buf() for _ in range(2)]
                wv, wB = wload_big(w_in_d[:, 5120 + blk * 512:5120 + (blk + 1) * 512], KC, 512)
                for tl in range(ntile):
                    s = tl % 2
                    kb.mm_group(pv[s][:], [(hsrc[:, k, col0 + tl * P:col0 + (tl + 1) * P], wv[:, k, :]) for k in range(KC)], [hsrcB] + wB, [pvB[s]])
                    kb.op("act", "activation", [pvB[s]], [vB], out=vdst[:, tl, :], in_=pv[s][:], func=AF.Copy)

        def hgrn_kq(R, hsrc, hsrcB, col0, blocks, want_q, qT, qB, kT, kB, kh, khB, ebC, ebCB, ebmap, hb, bw, tag, vspec=None, vrate=1):
            with contextlib.ExitStack() as s2:
                NSL = 2 if want_q else 3
                vjobs = []
                if vspec is not None:
                    ntile_v, vdst, vB_ = vspec
                    pv = [ps(s2, tag + "pv%d" % i, (P, 512), F32) for i in range(2)]
                    pvB = [Buf() for _ in range(2)]
                    wv, wvB = wload_big(w_in_d[:, 5120 + hb * 512:5120 + (hb + 1) * 512], KC, 512, key=tag + "hi%d" % hb)

                    def mk_v(tl):
                        def f_():
                            sv = tl % 2
                            kb.mm_group(pv[sv][:], [(hsrc[:, k, col0 + tl * P:col0 + (tl + 1) * P], wv[:, k, :]) for k in range(KC)], [hsrcB] + wvB, [pvB[sv]])
                            kb.op("dve", "tensor_copy", [pvB[sv]], [vB_], out=vdst[:, tl, :], in_=pv[sv][:])
                        return f_
                    vjobs = [mk_v(tl) for tl in range(ntile_v)]
                pf = [ps(s2, tag + "pf%d" % i, (P, 512), F32) for i in range(NSL)]
                pfB = [Buf() for _ in range(NSL)]
                pq = [ps(s2, tag + "pq%d" % i, (P, 512), F32) for i in range(NSL)] if want_q else []
                pqB = [Buf() for _ in range(NSL)]
                pt = [ps(s2, tag + "pt%d" % i, (P, 4, P), BF16) for i in range(2)]
                ptB = [Buf() for _ in range(2)]
                tmp = [[R.alloc((P, bw), F32) for j in range(5)] for i in range(NSL)]
                tmpB = [[Buf() for j in range(5)] for i in range(NSL)]
                kht = [R.alloc((P, bw), BF16) for i in range(NSL)]
                khtB = [Buf() for _ in range(NSL)]
                cnt = 0
                pend_t = []
                if not want_q:
                    wf_big, wfB_big = wload_big(w_in_d[:, 4096 + hb * 512:4096 + (hb + 1) * 512], KC, 512, key=tag + "hf%d" % hb)
                for hh in range(4):
                    h = hb * 4 + hh
                    if want_q:
                        if hh % 2 == 0:
                            cf = 4096 + hb * 512 + (hh // 2) * 256
                            cq = 3072 + hb * 512 + (hh // 2) * 256
                            wf_sm, wfB = wload_sm(w_in_d[:, cf:cf + 256], KC, 256, key=tag + "hf%d_%d" % (hb, hh // 2))
                            wq_sm, wqB = wload_sm(w_in_d[:, cq:cq + 256], KC, 256, key=tag + "hq%d_%d" % (hb, hh // 2))
                        wf = wf_sm[:, :, (hh % 2) * P:(hh % 2 + 1) * P]
                        wq = wq_sm[:, :, (hh % 2) * P:(hh % 2 + 1) * P]
                    else:
                        wf = wf_big[:, :, hh * P:(hh + 1) * P]
                        wfB = wfB_big
                    for (c0, n, segs) in blocks:
                        s = cnt % NSL
                        cnt += 1
                        sgn, ff, bb, eb, enb = [tmp[s][j][:, 0:n] for j in range(5)]
                        sgnB, ffB, bbB, ebB, enbB = tmpB[s]
                        kb.mm_group(pf[s][:, 0:n], [(wf[:, k, :], hsrc[:, k, col0 + c0:col0 + c0 + n]) for k in range(KC)], [hsrcB] + wfB, [pfB[s]])
                        if want_q:
                            kb.mm_group(pq[s][:, 0:n], [(wq[:, k, :], hsrc[:, k, col0 + c0:col0 + c0 + n]) for k in range(KC)], [hsrcB] + wqB, [pqB[s]])
                        while pend_t:
                            pend_t.pop(0)()
                        kb.op("act", "activation", [pfB[s]], [sgnB], out=sgn, in_=pf[s][:, 0:n], func=AF.Sigmoid, scale=-1.0)
                        kb.op("dve", "tensor_scalar", [sgnB, lbcB], [ffB], out=ff, in0=sgn, scalar1=lbc[:, 16 + h:17 + h], scalar2=1.0, op0=ALU.mult, op1=ALU.add)
                        kb.op("act", "activation", [ffB], [ffB], out=ff, in_=ff, func=AF.Ln)
                        kb.op("dve", "tensor_tensor_scan", [ffB, constsB], [bbB], out=bb, data0=consts[:, K_RST + c0:K_RST + c0 + n], data1=ff, initial=0.0, op0=ALU.mult, op1=ALU.add)
                        kb.op("act", "activation", [bbB], [ebB], out=eb, in_=bb, func=AF.Exp)
                        kb.op("act", "activation", [bbB], [enbB], out=enb, in_=bb, func=AF.Exp, scale=-1.0)
                        if want_q:
                            kb.op("dve", "scalar_tensor_tensor", [pqB[s], ebB], [qB], out=qT[:, hh, c0:c0 + n], in0=pq[s][:, 0:n], scalar=float(P) ** -0.5, in1=eb, op0=ALU.mult, op1=ALU.mult)
                        kb.op("dve", "scalar_tensor_tensor", [sgnB, enbB, lbcB], [kB], out=kT[:, hh, c0:c0 + n], in0=sgn, scalar=lbc[:, 8 + h:9 + h], in1=enb, op0=ALU.mult, op1=ALU.mult)
                        for (off, ln, L) in segs:
                            nch = ln // L
                            ch0 = ebmap[c0 + off]
                            ebv = eb[:, off:off + ln].rearrange("p (c l) -> p c l", l=L)[:, :, L - 1:L]
                            kb.op("act", "activation", [ebB], [ebCB], out=ebC[:, hh, ch0:ch0 + nch].unsqueeze(2), in_=ebv, func=AF.Copy)
                            kb.op("dve", "tensor_tensor", [kB, ebB], [khtB[s]], out=kht[s][:, off:off + ln].rearrange("p (c l) -> p c l", l=L),
                                  in0=kT[:, hh, c0 + off:c0 + off + ln].rearrange("p (c l) -> p c l", l=L), in1=ebv.to_broadcast([P, nch, L]), op=ALU.mult)
                        def mk_t(s=s, sp_=cnt % 2, n=n, c0=c0, hh=hh):
                            def f_():
                                ntl = n // P
                                kb.transposes([(pt[sp_][:, i, :], kht[s][:, i * P:(i + 1) * P], ident_bf) for i in range(ntl)], [khtB[s], identB], [ptB[sp_]])
                                t0 = c0 // P
                                kb.op("act", "activation", [ptB[sp_]], [khB], out=kh[:, t0:t0 + ntl, hh * P:(hh + 1) * P], in_=pt[sp_][:, 0:ntl, :], func=AF.Copy)
                            return f_
                        pend_t.append(mk_t())
                        for _ in range(vrate):
                            if vjobs:
                                vjobs.pop(0)()
                while pend_t:
                    pend_t.pop(0)()
                while vjobs:
                    vjobs.pop(0)()

        def s_update(U_ap, UB, h, ebc_ap, ebcB):
            cur = cur_h[h]
            nxt = 1 - cur
            kb.op("dve", "scalar_tensor_tensor", [SfB[cur][h], UB, ebcB], [SfB[nxt][h]], out=Sf[nxt][:, h, :], in0=Sf[cur][:, h, :], scalar=ebc_ap, in1=U_ap, op0=ALU.mult, op1=ALU.add)
            kb.op("act", "activation", [SfB[nxt][h]], [SbB[nxt][h]], out=Sb[nxt][:, h, :], in_=Sf[nxt][:, h, :], func=AF.Copy)
            cur_h[h] = nxt

        RX = Region(58 * KIB, 200 * KIB)
        with contextlib.ExitStack() as st:
            hTp = RX.alloc((P, KC, TP), BF16)
            hTpB = Buf()
            mark = RX.p
            with contextlib.ExitStack() as s1:
                jobs = [(xp_d[i * P:(i + 1) * P, :], C_GPRE, hTp[:, :, i * P:(i + 1) * P], hTpB) for i in range(TP // P)]
                norm_phase(RX, s1, jobs, "np")
            kb.soft_barrier()
            RX.p = mark
            kb.op("dve", "tensor_copy", [hTpB], [tailB], out=tailT, in_=hTp[:, :, TP - TAIL:TP])
            ebmap_p = {c * 64: c for c in range(16)}
            blocks_p = [(0, 512, [(0, 512, 64)]), (512, 512, [(0, 512, 64)])]
            for hb in range(2):
                RX.p = mark
                vp = RX.alloc((P, 8, 512), BF16)
                khp = RX.alloc((P, 8, 512), BF16)
                kTp = RX.alloc((P, 4, TP), BF16)
                ebCp = RX.alloc((P, 4, 16), F32)
                vpB, khpB, kTpB, ebCpB = Buf(), Buf(), Buf(), Buf()
                hgrn_kq(RX, hTp, hTpB, 0, blocks_p, False, None, None, kTp, kTpB, khp, khpB, ebCp, ebCpB, ebmap_p, hb, 512, "p", vspec=(8, vp, vpB), vrate=1)
                if hb == 0:
                    wcache["phi1"] = wload_big(w_in_d[:, 5120 + 512:5120 + 1024], KC, 512)
                    wcache["phf1"] = wload_big(w_in_d[:, 4096 + 512:4096 + 1024], KC, 512)
                else:
                    wcache["mhi0"] = wload_big(w_in_d[:, 5120:5120 + 512], KC, 512)
                    wcache["mhf0_0"] = wload_sm(w_in_d[:, 4096:4096 + 256], KC, 256)
                    wcache["mhq0_0"] = wload_sm(w_in_d[:, 3072:3072 + 256], KC, 256)
                with contextlib.ExitStack() as s2:
                    pu = [ps(s2, "ppu%d" % i, (P, 4, P), F32) for i in range(4)]
                    puB = [Buf() for i in range(4)]
                    cnt = 0
                    for tl in range(8):
                        for c in range(2):
                            s = cnt % 4
                            cnt += 1
                            for hh in range(4):
                                kb.mm_group(pu[s][:, hh, :], [(khp[c * 64:(c + 1) * 64, tl, hh * P:(hh + 1) * P], vp[c * 64:(c + 1) * 64, tl, hh * P:(hh + 1) * P])], [khpB, vpB], [puB[s]])
                            for hh in range(4):
                                ch = tl * 2 + c
                                s_update(pu[s][:, hh, :], puB[s], hb * 4 + hh, ebCp[:, hh, ch:ch + 1], ebCpB)
                kb.barrier()

        RF = Region(149 * KIB, 200 * KIB)
        with contextlib.ExitStack() as st:
            kb.op("dve", "tensor_copy", [tailB], [hTB], out=hT[:, :, 0:TAIL], in_=tailT)
            jobs = [(x_d[i * P:(i + 1) * P, :], C_GPRE, hT[:, :, TAIL + i * P:TAIL + (i + 1) * P], hTB) for i in range(T // P)]
            norm_phase(RF, st, jobs, "nm")
        kb.soft_barrier()

        RH = Region(113 * KIB, 200 * KIB)
        ebmap_m = {c * 64: c for c in range(16)}
        for n_ in range(NSEQ):
            ebmap_m[TP + n_ * LS] = 16 + n_
        blocks_m = [(0, 384, [(0, 384, 64)]), (384, 384, [(0, 384, 64)]), (768, 384, [(0, 256, 64), (256, 128, LS)])]
        hgsS = kb.dsem()
        for hb in range(2):
            with contextlib.ExitStack() as st:
                RH.reset()
                v_h = RH.alloc((P, 9, 512), BF16)
                kh_h = RH.alloc((P, 9, 512), BF16)
                qT_h = RH.alloc((P, 4, T), BF16)
                kT_h = RH.alloc((P, 4, T), BF16)
                ebC_h = RH.alloc((P, 4, 32), F32)
                slh = RH.alloc((P, 4, T), BF16)
                slhB = Buf()
                vB, khB, qB, kB_, ebCB = Buf(), Buf(), Buf(), Buf(), Buf()
                mark = RH.p
                hgrn_kq(RH, hT, hTB, TAIL, blocks_m, True, qT_h, qB, kT_h, kB_, kh_h, khB, ebC_h, ebCB, ebmap_m, hb, 384, "m", vspec=(9, v_h, vB), vrate=2)
                kb.soft_barrier()
                RH.p = mark
                with contextlib.ExitStack() as s2:
                    pat = ps(s2, "pat", (P, 4, P), F32)
                    pu = [ps(s2, "pu%d" % i, (P, 4, P), F32) for i in range(2)]
                    po2 = [ps(s2, "po%d" % i, (P, 4, P), F32) for i in range(1)] * 2
                    pg = [ps(s2, "hpg%d" % i, (P, 512), F32) for i in range(2)]
                    pgB = [Buf(), Buf()]
                    pss = ps(s2, "pss", (P, 512), F32)
                    pos = ps(s2, "pos", (P, 4, P), F32)
                    patB, pssB, posB = Buf(), Buf(), Buf()
                    po2B = [Buf()] * 2
                    pend_r = []
                    wg, wgB = wload_big(w_in_d[:, 6144 + hb * 512:6144 + (hb + 1) * 512], KC, 512)

                    def mk_hog(hh, tb, idx):
                        def f_():
                            sg_ = idx % 2
                            c0 = tb * 384
                            kb.mm_group(pg[sg_][:, 0:384], [(wg[:, k, hh * P:(hh + 1) * P], hT[:, k, TAIL + c0:TAIL + c0 + 384]) for k in range(KC)], [hTB] + wgB, [pgB[sg_]])
                            kb.op("act", "activation", [pgB[sg_]], [slhB], out=slh[:, hh, c0:c0 + 384], in_=pg[sg_][:, 0:384], func=AF.Silu)
                        return f_
                    hogjobs = [mk_hog(hh, tb, tb * 4 + hh) for tb in range(3) for hh in range(4)]
                    puB = [Buf(), Buf()]
                    AT = [RH.alloc((P, 4, P), BF16) for i in range(2)]
                    ATB = [Buf(), Buf()]
                    sq = RH.alloc((P, 512), BF16)
                    sqB = Buf()
                    rr = RH.alloc((P, 512), F32)
                    rrB = Buf()
                    osb = RH.alloc((P, 4, P), F32)
                    osbB = Buf()
                    NSS = 4
                    S0f = [RH.alloc((P, 4, P), F32) for i in range(NSS)]
                    S0b = [RH.alloc((P, 4, P), BF16) for i in range(NSS)]
                    So = [RH.alloc((P, 4, P), F32) for i in range(NSS)]
                    vmk = [RH.alloc((P, 512), BF16) for i in range(NSS)]
                    S0fB, S0bB, SoB, vmkB = [[Buf() for _ in range(NSS)] for _ in range(4)]
                    S0fS, S0bS, SoS = [[kb.dsem() for _ in range(NSS)] for _ in range(3)]

                    def rmsnorm_o(o_ap, oB, tl, defer=False):
                        kb.op("act", "activation", [oB], [sqB], out=sq, in_=o_ap.rearrange("p a b -> p (a b)"), func=AF.Square)

                        def tail():
                            kb.mm_group(pss[:], [(ones_bf, sq)], [onesB, sqB], [pssB])
                            kb.op("act", "activation", [pssB], [rrB], out=rr, in_=pss[:], func=AF.Ln, scale=1.0 / P, bias=EPS)
                            kb.op("act", "activation", [rrB], [rrB], out=rr, in_=rr, func=AF.Exp, scale=-0.5)
                            for hh in range(4):
                                h = hb * 4 + hh
                                kb.op("dve", "scalar_tensor_tensor", [oB, rrB, colsB], [obB], out=ob[:, h, tl * P:(tl + 1) * P], in0=o_ap[:, hh, :], scalar=cols[:, C_GHG + h:C_GHG + h + 1], in1=rr[:, hh * P:(hh + 1) * P], op0=ALU.mult, op1=ALU.mult)
                        if defer:
                            pend_r.append(tail)
                        else:
                            tail()

                    for tl in range(9):
                        sa = tl % 2
                        po = po2[tl % 2]
                        poB = po2B[tl % 2]
                        msk = mask_pr if tl < 8 else mask_sm
                        for _ in range(2):
                            if hogjobs:
                                hogjobs.pop(0)()
                        for hh in range(4):
                            kb.mm_group(pat[:, hh, :], [(kT_h[:, hh, tl * P:(tl + 1) * P], qT_h[:, hh, tl * P:(tl + 1) * P])], [kB_, qB], [patB])
                        kb.op("dve", "tensor_tensor", [patB, constsB], [ATB[sa]], out=AT[sa], in0=pat[:], in1=msk.unsqueeze(1).to_broadcast([P, 4, P]), op=ALU.mult)
                        if tl < 8:
                            for c in range(2):
                                for hh in range(4):
                                    kb.mm_group(pu[c][:, hh, :], [(kh_h[c * 64:(c + 1) * 64, tl, hh * P:(hh + 1) * P], v_h[c * 64:(c + 1) * 64, tl, hh * P:(hh + 1) * P])], [khB, vB], [puB[c]])
                            before = [cur_h[hb * 4 + hh] for hh in range(4)]
                            for hh in range(4):
                                s_update(pu[0][:, hh, :], puB[0], hb * 4 + hh, ebC_h[:, hh, 2 * tl:2 * tl + 1], ebCB)
                            for hh in range(4):
                                h = hb * 4 + hh
                                b0 = before[hh]
                                b1 = 1 - b0
                                kb.mm_group(None, [
                                    (po[:, hh, :], v_h[:, tl, hh * P:(hh + 1) * P], AT[sa][:, hh, :]),
                                    (po[:, hh, 0:64], Sb[b0][:, h, :], qT_h[:, hh, tl * P:tl * P + 64]),
                                    (po[:, hh, 64:128], Sb[b1][:, h, :], qT_h[:, hh, tl * P + 64:(tl + 1) * P]),
                                ], [vB, ATB[sa], SbB[b0][h], SbB[b1][h], qB], [poB])
                            while pend_r:
                                pend_r.pop(0)()
                            for hh in range(4):
                                s_update(pu[1][:, hh, :], puB[1], hb * 4 + hh, ebC_h[:, hh, 2 * tl + 1:2 * tl + 2], ebCB)
                            rmsnorm_o(po[:], poB, tl)
                            if tl == 7:
                                for hh in range(4):
                                    h = hb * 4 + hh
                                    kb.dma("sp", hgsS, hgp_d[h], Sf[cur_h[h]][:, h, :], [SfB[cur_h[h]][h]], [])
                        else:
                            if hb == 0:
                                wcache["mhi1"] = wload_big(w_in_d[:, 5120 + 512:5120 + 1024], KC, 512)
                                wcache["mhf1_0"] = wload_sm(w_in_d[:, 4096 + 512:4096 + 768], KC, 256)
                                wcache["mhq1_0"] = wload_sm(w_in_d[:, 3072 + 512:3072 + 768], KC, 256)
                            else:
                                wcache["cv0"] = wload_sm(w_in_d[:, 0:256], KC, 256)
                                wcache["cg0"] = wload_sm(w_in_d[:, 1024:1280], KC, 256)
                            for hh in range(4):
                                kb.mm_group(po[:, hh, :], [(v_h[:, tl, hh * P:(hh + 1) * P], AT[sa][:, hh, :])], [vB, ATB[sa]], [poB])
                            while pend_r:
                                pend_r.pop(0)()
                            def ld_state(m_):
                                sm_ = m_ % NSS
                                src_ = shg_d[m_, hb * 4:(hb + 1) * 4].rearrange("h k v -> k h v")
                                kb.dma("sp", S0fS[sm_], S0f[sm_], src_, [], [S0fB[sm_]])
                                kb.dma("pool", S0bS[sm_], S0b[sm_], src_, [], [S0bB[sm_]])
                            for m_ in range(NSS - 1):
                                ld_state(m_)
                            for n_ in range(NSEQ):
                                s = n_ % NSS
                                sq_ = n_ % 2
                                if n_ + NSS - 1 < NSEQ:
                                    ld_state(n_ + NSS - 1)
                                for hh in range(4):
                                    kb.mm_group(pos[:, hh, n_ * LS:(n_ + 1) * LS], [(S0b[s][:, hh, :], qT_h[:, hh, TP + n_ * LS:TP + (n_ + 1) * LS])], [S0bB[s], qB], [posB])
                                kb.op("act", "activation", [vB, constsB], [vmkB[s]], out=vmk[s], in_=v_h[:, tl, :], func=AF.Copy, scale=consts[:, K_SEL + n_:K_SEL + n_ + 1])
                                for hh in range(4):
                                    kb.mm_group(pu[sq_][:, hh, :], [(kh_h[:, tl, hh * P:(hh + 1) * P], vmk[s][:, hh * P:(hh + 1) * P])], [khB, vmkB[s]], [puB[sq_]])
                                for hh in range(4):
                                    kb.op("dve", "scalar_tensor_tensor", [S0fB[s], puB[sq_], ebCB], [SoB[s]], out=So[s][:, hh, :], in0=S0f[s][:, hh, :], scalar=ebC_h[:, hh, 16 + n_:17 + n_], in1=pu[sq_][:, hh, :], op0=ALU.mult, op1=ALU.add)
                                kb.dma("sp", SoS[s], hgs_d[n_, hb * 4:(hb + 1) * 4].rearrange("h k v -> k h v"), So[s], [SoB[s]], [])
                            kb.op("act", "activation", [posB], [osbB], out=osb, in_=pos[:], func=AF.Copy)
                            kb.op("dve", "tensor_tensor", [poB, osbB], [osbB], out=osb, in0=po[:], in1=osb, op=ALU.add)
                            rmsnorm_o(osb, osbB, tl)
                    while hogjobs:
                        hogjobs.pop(0)()
                    for hh in range(4):
                        h = hb * 4 + hh
                        for tb in range(3):
                            c0 = tb * 384
                            kb.op("dve", "tensor_tensor", [obB, slhB], [obB], out=ob[:, h, c0:c0 + 384], in0=ob[:, h, c0:c0 + 384], in1=slh[:, hh, c0:c0 + 384], op=ALU.mult)
                kb.barrier()

        if "dbg" in phases:
            kb.dma("sp", kb.dsem(), dbg["ob"], ob, [obB], [])

        if ALL or "conv" in phases:
            RC = Region(131 * KIB, 200 * KIB)
            RS2 = Region(14 * KIB, 26 * KIB)
            with contextlib.ExitStack() as st:
                u_p = RC.alloc((P, 8, TAIL + TP), BF16)
                u_s = RC.alloc((P, 8, NSEQ, 38), BF16)
                ufp_s = RC.alloc((P, 8, P), F32)
                ufp_t = RC.alloc((P, 8, TAIL), F32)
                dcb = RC.alloc((P, 8, T), BF16)
                S1 = RS2.alloc((P, T), F32)
                S2 = RS2.alloc((P, T), F32)
                upB, usB, ufsB, uftB, dcbB, S1B, S2B = Buf(), Buf(), Buf(), Buf(), Buf(), Buf(), Buf()
                mark = RC.p
                with contextlib.ExitStack() as s2:
                    tl_in = [RC.alloc((120, 1024), F32) for i in range(2)]
                    tlB = [Buf(), Buf()]
                    tlS = [kb.dsem(), kb.dsem()]
                    ptl = [ps(s2, "ptl%d" % i, (P, 8, P), F32) for i in range(2)]
                    ptlB = [Buf(), Buf()]
                    for g in range(4):
                        s = g % 2
                        kb.dma("sp", tlS[s], tl_in[s], sconv_d[4 * g:4 * g + 4].rearrange("n r c -> (n r) c"), [], [tlB[s]])
                        kb.transposes([(ptl[s][:, c, 0:120], tl_in[s][:, c * P:(c + 1) * P], ident_f[0:120, 0:120]) for c in range(8)], [tlB[s], constsB], [ptlB[s]])
                        for c in range(8):
                            kb.op("act", "activation", [ptlB[s]], [usB], out=u_s[:, c, 4 * g:4 * g + 4, 0:30], in_=ptl[s][:, c, 0:120].rearrange("p (n r) -> p n r", n=4), func=AF.Copy)
                    cs_S = kb.dsem()
                    kb.dma("sp", cs_S, convs_d[:, 0:22, :], sconv_d[:, 8:30, :], [], [])
                kb.soft_barrier()
                RC.p = mark
                with contextlib.ExitStack() as s2:
                    pv = [ps(s2, "cpv%d" % i, (P, 512), F32) for i in range(2)]
                    pg = [ps(s2, "cpg%d" % i, (P, 512), F32) for i in range(2)]
                    pvB, pgB = [Buf(), Buf()], [Buf(), Buf()]
                    sg = [RC.alloc((P, 416), F32) for i in range(2)]
                    sgB = [Buf(), Buf()]
                    cblocks = [(0, 384), (384, 384), (768, 416)]
                    cnt = 0
                    for cp in range(4):
                        wv, wvB = wload_sm(w_in_d[:, cp * 256:(cp + 1) * 256], KC, 256, key="cv%d" % cp)
                        wg, wgB = wload_sm(w_in_d[:, 1024 + cp * 256:1024 + (cp + 1) * 256], KC, 256, key="cg%d" % cp)
                        for ci in range(2):
                            c = cp * 2 + ci
                            for (c0, n) in cblocks:
                                s = cnt % 2
                                cnt += 1
                                kb.mm_group(pv[s][:, 0:n], [(wv[:, k, ci * P:(ci + 1) * P], hT[:, k, c0:c0 + n]) for k in range(KC)], [hTB] + wvB, [pvB[s]])
                                kb.mm_group(pg[s][:, 0:n], [(wg[:, k, ci * P:(ci + 1) * P], hT[:, k, c0:c0 + n]) for k in range(KC)], [hTB] + wgB, [pgB[s]])
                                kb.op("act", "activation", [pgB[s]], [sgB[s]], out=sg[s][:, 0:n], in_=pg[s][:, 0:n], func=AF.Sigmoid)
                                if c0 < 768:
                                    kb.op("dve", "tensor_tensor", [pvB[s], sgB[s]], [upB], out=u_p[:, c, c0:c0 + n], in0=pv[s][:, 0:n], in1=sg[s][:, 0:n], op=ALU.mult)
                                else:
                                    kb.op("dve", "tensor_tensor", [pvB[s], sgB[s]], [upB], out=u_p[:, c, 768:1056], in0=pv[s][:, 0:288], in1=sg[s][:, 0:288], op=ALU.mult)
                                    kb.op("dve", "tensor_tensor", [pvB[s], sgB[s]], [uftB], out=ufp_t[:, c, :], in0=pv[s][:, 256:288], in1=sg[s][:, 256:288], op=ALU.mult)
                                    kb.op("dve", "tensor_tensor", [pvB[s], sgB[s]], [ufsB], out=ufp_s[:, c, :], in0=pv[s][:, 288:416], in1=sg[s][:, 288:416], op=ALU.mult)
                                    kb.op("act", "activation", [ufsB], [usB], out=u_s[:, c, :, 30:38], in_=ufp_s[:, c, :].rearrange("p (n t) -> p n t", t=LS), func=AF.Copy)
                kb.soft_barrier()
                RC.p = mark
                with contextlib.ExitStack() as s2:
                    pts = ps(s2, "cpts", (P, 8, P), F32)
                    ptt = ps(s2, "cptt", (32, 8, P), F32)
                    ptsB, pttB = Buf(), Buf()
                    us_tm = RC.alloc((P, 1024), F32)
                    ut_tm = RC.alloc((32, 1024), F32)
                    ustB, uttB = Buf(), Buf()
                    kb.transposes([(pts[:, c, :], ufp_s[:, c, :], ident_f) for c in range(8)], [ufsB, constsB], [ptsB])
                    kb.op("act", "activation", [ptsB], [ustB], out=us_tm, in_=pts[:].rearrange("p a b -> p (a b)"), func=AF.Copy)
                    kb.transposes([(ptt[:, c, :], ufp_t[:, c, :], ident_f) for c in range(8)], [uftB, constsB], [pttB])
                    kb.op("act", "activation", [pttB], [uttB], out=ut_tm, in_=ptt[:].rearrange("p a b -> p (a b)"), func=AF.Copy)
                    for n_ in range(NSEQ):
                        kb.dma("sp", cs_S, convs_d[n_, 22:30, :], us_tm[n_ * LS:(n_ + 1) * LS, :], [ustB], [])
                    kb.dma("sp", cs_S, convp_d[:, :], ut_tm, [uttB], [])
                kb.soft_barrier()
                RC.p = mark
                with contextlib.ExitStack() as s2:
                    dg = [RC.alloc((P, 31, P), BF16) for i in range(2)]
                    dgB = [Buf(), Buf()]
                    pc = [ps(s2, "cpc%d" % i, (P, 512), F32) for i in range(3)]
                    pcB = [Buf(), Buf(), Buf()]
                    pst = [ps(s2, "cpst%d" % i, (P, 512), F32) for i in range(2)]
                    pstB = [Buf(), Buf()]
                    sqc = [RC.alloc((P, 512), BF16) for i in range(2)]
                    sqcB = [Buf(), Buf()]
                    tbs = [(0, 512), (512, 512), (1024, 128)]
                    cnt = 0
                    def build_dg(c):
                        d = c % 2
                        wcol = cols[:, C_CONVW + c:C_CONVW + c + 31 * 8 - 7:8]
                        kb.op("dve", "tensor_tensor", [identB, colsB], [dgB[d]], out=dg[d], in0=ident_bf.unsqueeze(1).to_broadcast([P, 31, P]), in1=wcol.unsqueeze(2).to_broadcast([P, 31, P]), op=ALU.mult)
                    build_dg(0)
                    for c in range(8):
                        d = c % 2
                        if c + 1 < 8:
                            build_dg(c + 1)
                        for ti, (t0, n) in enumerate(tbs):
                            s = cnt % 3
                            s2_ = cnt % 2
                            cnt += 1
                            if ti < 2:
                                pairs = [(dg[d][:, j, :], u_p[:, c, t0 + 2 + j:t0 + 2 + j + n]) for j in range(31)]
                                outap = pc[s][:, 0:n]
                            else:
                                pairs = [(dg[d][:, j, :], u_s[:, c, :, j:j + LS]) for j in range(31)]
                                outap = pc[s][:, 0:n].rearrange("p (a b) -> p a b", b=LS)
                            kb.mm_group(outap, pairs, [dgB[d], upB, usB], [pcB[s]])
                            kb.op("act", "activation", [pcB[s], colsB], [dcbB], out=dcb[:, c, t0:t0 + n], in_=pc[s][:, 0:n], func=AF.Identity, bias=cols[:, C_CONVB + c:C_CONVB + c + 1])
                            kb.op("act", "activation", [dcbB], [sqcB[s2_]], out=sqc[s2_][:, 0:n], in_=dcb[:, c, t0:t0 + n], func=AF.Square)
                            kb.mm_group(pst[0][:, 0:n], [(ones_bf, dcb[:, c, t0:t0 + n])], [onesB, dcbB], [pstB[0]])
                            kb.mm_group(pst[1][:, 0:n], [(ones_bf, sqc[s2_][:, 0:n])], [onesB, sqcB[s2_]], [pstB[1]])
                            if c == 0:
                                kb.op("dve", "tensor_copy", [pstB[0]], [S1B], out=S1[:, t0:t0 + n], in_=pst[0][:, 0:n])
                                kb.op("dve", "tensor_copy", [pstB[1]], [S2B], out=S2[:, t0:t0 + n], in_=pst[1][:, 0:n])
                            else:
                                kb.op("dve", "tensor_tensor", [pstB[0], S1B], [S1B], out=S1[:, t0:t0 + n], in0=pst[0][:, 0:n], in1=S1[:, t0:t0 + n], op=ALU.add)
                                kb.op("dve", "tensor_tensor", [pstB[1], S2B], [S2B], out=S2[:, t0:t0 + n], in0=pst[1][:, 0:n], in1=S2[:, t0:t0 + n], op=ALU.add)
                kb.soft_barrier()
                RC.p = mark
                with contextlib.ExitStack() as s2:
                    msq = RC.alloc((P, T), F32)
                    msqB = Buf()
                    t1 = [RC.alloc((P, 384), F32) for i in range(2)]
                    t1B = [Buf(), Buf()]
                    kb.op("dve", "tensor_scalar", [S1B], [S1B], out=S1, in0=S1, scalar1=1.0 / 1024, scalar2=None, op0=ALU.mult)
                    kb.op("dve", "tensor_tensor", [S1B], [msqB], out=msq, in0=S1, in1=S1, op=ALU.mult)
                    kb.op("dve", "scalar_tensor_tensor", [S2B, msqB], [S2B], out=S2, in0=S2, scalar=1.0 / 1024, in1=msq, op0=ALU.mult, op1=ALU.subtract)
                    kb.op("act", "activation", [S2B], [S2B], out=S2, in_=S2, func=AF.Ln, bias=EPS)
                    kb.op("act", "activation", [S2B], [S2B], out=S2, in_=S2, func=AF.Exp, scale=-0.5)
                    pg = [ps(s2, "csg%d" % i, (P, 512), F32) for i in range(2)]
                    pgB = [Buf(), Buf()]
                    sl = [RC.alloc((P, 384), BF16) for i in range(2)]
                    slB = [Buf(), Buf()]
                    cnt = 0
                    for cb in range(2):
                        wg, wgB = wload_big(w_in_d[:, 2048 + cb * 512:2048 + (cb + 1) * 512], KC, 512)
                        if cb == 1 and (ALL or "attn" in phases):
                            wcache["kv0"] = wload_big(w_kv_d[:, 0:512], KC, 512)
                        for ci in range(4):
                            c = cb * 4 + ci
                            for tb in range(3):
                                s = cnt % 2
                                cnt += 1
                                c0 = tb * 384
                                kb.mm_group(pg[s][:, 0:384], [(wg[:, k, ci * P:(ci + 1) * P], hT[:, k, TAIL + c0:TAIL + c0 + 384]) for k in range(KC)], [hTB] + wgB, [pgB[s]])
                                kb.op("dve", "tensor_tensor", [dcbB, S1B], [t1B[s]], out=t1[s], in0=dcb[:, c, c0:c0 + 384], in1=S1[:, c0:c0 + 384], op=ALU.subtract)
                                kb.op("dve", "tensor_tensor", [t1B[s], S2B], [t1B[s]], out=t1[s], in0=t1[s], in1=S2[:, c0:c0 + 384], op=ALU.mult)
                                kb.op("act", "activation", [t1B[s], colsB], [aB], out=a_[:, c, c0:c0 + 384], in_=t1[s], func=AF.Silu, scale=cols[:, C_LNG + c:C_LNG + c + 1], bias=cols[:, C_LNB + c:C_LNB + c + 1])
                                kb.op("act", "activation", [pgB[s]], [slB[s]], out=sl[s], in_=pg[s][:, 0:384], func=AF.Silu)
                                kb.op("dve", "tensor_tensor", [aB, slB[s]], [aB], out=a_[:, c, c0:c0 + 384], in0=a_[:, c, c0:c0 + 384], in1=sl[s], op=ALU.mult)
                kb.barrier()
            if "dbg" in phases:
                kb.dma("sp", kb.dsem(), dbg["a"], a_, [aB], [])

        alvl = 9
        for p_ in phases:
            if p_.startswith("attn="):
                alvl = int(p_.split("=")[1])
        if ALL or "attn" in phases or alvl < 9:
            RA = Region(149 * KIB, 200 * KIB)
            RS2 = Region(14 * KIB, 26 * KIB)
            with contextlib.ExitStack() as st:
                memT = RS2.alloc((P, KC, 256), BF16)
                memTB = Buf()
                mark0 = RA.p
                with contextlib.ExitStack() as s1:
                  if alvl >= 1:
                    jobs = [(mem_d[i * P:(i + 1) * P, :], C_GMEM, memT[:, :, i * P:(i + 1) * P], memTB) for i in range(2)]
                    norm_phase(RA, s1, jobs, "na")
                kb.soft_barrier()
                RA.p = mark0
                qTa = RA.alloc((P, 8, T), BF16)
                mark = RA.p
                KTp = RA.alloc((P, 8, 256), BF16)
                Vp = RA.alloc((P, 2, 1024), BF16)
                qTaB, KTpB, VpB = Buf(), Buf(), Buf()
                mark2 = RA.p
                with contextlib.ExitStack() as s2:
                  if alvl >= 2:
                    pk = [ps(s2, "apk%d" % i, (P, 512), F32) for i in range(2)]
                    pkB = [Buf(), Buf()]
                    pkt = [ps(s2, "apkt%d" % i, (P, 256), F32) for i in range(2)]
                    pktB = [Buf(), Buf()]
                    stg = [RA.alloc((P, 512), F32) for i in range(2)]
                    stgB = [Buf(), Buf()]
                    stgS = [kb.dsem(), kb.dsem()]
                    cnt = 0
                    cnt2 = 0
                    nblk = 4
                    for p_ in phases:
                        if p_.startswith("blk="):
                            nblk = int(p_.split("=")[1])
                    for blk in range(nblk):
                        wk, wkB = wload_big(w_kv_d[:, blk * 512:(blk + 1) * 512], KC, 512, key="kv%d" % blk)
                        isk = blk < 2
                        dst_d = mk_d if isk else mv_d
                        cb = blk % 2
                        for mt in range(2):
                            s = cnt % 2
                            cnt += 1
                            if "nomm" not in phases:
                                kb.mm_group(pk[s][:], [(memT[:, k, mt * P:(mt + 1) * P], wk[:, k, :]) for k in range(KC)], [memTB] + wkB, [pkB[s]])
                            if "noact" not in phases:
                                kb.op("act", "activation", [pkB[s]], [stgB[s]], out=stg[s], in_=pk[s][:], func=AF.Copy)
                                if "skipst" not in phases:
                                    kb.dma("sp", stgS[s], dst_d[mt * P:(mt + 1) * P, cb * 512:(cb + 1) * 512], stg[s], [stgB[s]], [])
                            if not isk and "novp" not in phases:
                                kb.op("dve", "tensor_copy", [stgB[s]], [VpB], out=Vp[:, mt, cb * 512:(cb + 1) * 512], in_=stg[s])
                        if isk and "skipkt" not in phases:
                            for dc in range(4):
                                s = cnt2 % 2
                                cnt2 += 1
                                kb.mm_group(pkt[s][:], [(wk[:, k, dc * P:(dc + 1) * P], memT[:, k, :]) for k in range(KC)], [memTB] + wkB, [pktB[s]])
                                kb.op("dve", "tensor_copy", [pktB[s]], [KTpB], out=KTp[:, cb * 4 + dc, :], in_=pkt[s][:])
                kb.soft_barrier()
                RA.p = mark2
                with contextlib.ExitStack() as s2:
                  if alvl >= 3:
                    pq = [ps(s2, "apq%d" % i, (P, 512), F32) for i in range(2)]
                    pqB = [Buf(), Buf()]
                    cnt = 0
                    for cb in range(2):
                        wq, wqB = wload_big(w_in_d[:, 7168 + cb * 512:7168 + (cb + 1) * 512], KC, 512)
                        for ci in range(4):
                            c = cb * 4 + ci
                            for tb in range(3):
                                s = cnt % 2
                                cnt += 1
                                c0 = tb * 384
                                kb.mm_group(pq[s][:, 0:384], [(wq[:, k, ci * P:(ci + 1) * P], hT[:, k, TAIL + c0:TAIL + c0 + 384]) for k in range(KC)], [hTB] + wqB, [pqB[s]])
                                kb.op("act", "activation", [pqB[s]], [qTaB], out=qTa[:, c, c0:c0 + 384], in_=pq[s][:, 0:384], func=AF.Copy)
                kb.soft_barrier()
                SC = float(256) ** -0.5
                with contextlib.ExitStack() as s2:
                  if alvl >= 4:
                    sc = [ps(s2, "asc%d" % i, (P, 512), F32) for i in range(2)]
                    scB = [Buf(), Buf()]
                    pden = ps(s2, "aden", (P, 512), F32)
                    pdenB = Buf()
                    pnum = [ps(s2, "anum%d" % i, (P, 512), F32) for i in range(2)]
                    pnumB = [Buf(), Buf()]
                    ET = [RA.alloc((P, 512), BF16) for i in range(2)]
                    ETB = [Buf(), Buf()]
                    rden = RA.alloc((P, 512), F32)
                    rdenB = Buf()
                    for tb in range(2):
                        for h in range(4):
                            for mt in range(2):
                                kb.mm_group(sc[mt][:], [(KTp[:, 2 * h + dc, mt * P:(mt + 1) * P], qTa[:, 2 * h + dc, tb * 512:(tb + 1) * 512]) for dc in range(2)], [KTpB, qTaB], [scB[mt]])
                                kb.op("act", "activation", [scB[mt]], [ETB[mt]], out=ET[mt], in_=sc[mt][:], func=AF.Exp, scale=SC)
                            kb.mm_group(pden[:], [(ones_bf, ET[0]), (ones_bf, ET[1])], [onesB, ETB[0], ETB[1]], [pdenB])
                            kb.op("dve", "reciprocal", [pdenB], [rdenB], out=rden, in_=pden[:])
                            for dc in range(2):
                                kb.mm_group(pnum[dc][:], [(Vp[:, mt, (2 * h + dc) * P:(2 * h + dc + 1) * P], ET[mt]) for mt in range(2)], [VpB, ETB[0], ETB[1]], [pnumB[dc]])
                                kb.op("dve", "tensor_tensor", [pnumB[dc], rdenB], [aoB], out=ao[:, 2 * h + dc, tb * 512:(tb + 1) * 512], in0=pnum[dc][:], in1=rden, op=ALU.mult)
                kb.soft_barrier()
                RA.p = mark
                with contextlib.ExitStack() as s2:
                  if alvl >= 5:
                    Kn = [RA.alloc((P, 2, 1024), BF16) for i in range(2)]
                    Vn = [RA.alloc((P, 2, 1024), BF16) for i in range(2)]
                    KTn = [RA.alloc((P, 8, 256), BF16) for i in range(2)]
                    KnB, VnB, KTnB = [Buf(), Buf()], [Buf(), Buf()], [Buf(), Buf()]
                    KnS, VnS = [kb.dsem(), kb.dsem()], [kb.dsem(), kb.dsem()]
                    ETs = RA.alloc((P, 2, NSEQ, 4, LS), BF16)
                    ETsB = Buf()
                    rds = RA.alloc((P, NSEQ, 4, LS), F32)
                    rdsB = Buf()
                    pkt2 = [ps(s2, "skt%d" % i, (P, 8, 256), BF16) for i in range(1)]
                    pkt2B = [Buf()]
                    scs = [ps(s2, "sscs%d" % i, (P, 512), F32) for i in range(2)]
                    scsB = [Buf(), Buf()]
                    nums = ps(s2, "snum", (P, 8, P), F32)
                    numsB = Buf()
                    dens = ps(s2, "sden", (P, 512), F32)
                    densB = Buf()
                    for n_ in range(NSEQ):
                        s = n_ % 2
                        kb.dma("pool", KnS[s], Kn[s], kc_d[n_].rearrange("(mt p) f -> p mt f", p=P), [], [KnB[s]])
                        kb.dma("pool", VnS[s], Vn[s], vc_d[n_].rearrange("(mt p) f -> p mt f", p=P), [], [VnB[s]])
                        kb.transposes([(pkt2[0][:, ch, mt * P:(mt + 1) * P], Kn[s][:, mt, ch * P:(ch + 1) * P], ident_bf) for ch in range(8) for mt in range(2)], [KnB[s], identB], [pkt2B[0]])
                        kb.op("act", "activation", [pkt2B[0]], [KTnB[s]], out=KTn[s][:, 0:4, :], in_=pkt2[0][:, 0:4, :], func=AF.Copy)
                        kb.op("dve", "tensor_copy", [pkt2B[0]], [KTnB[s]], out=KTn[s][:, 4:8, :], in_=pkt2[0][:, 4:8, :])
                        for h in range(4):
                            for mt in range(2):
                                kb.mm_group(scs[s][:, (mt * 4 + h) * LS:(mt * 4 + h + 1) * LS], [(KTn[s][:, 2 * h + dc, mt * P:(mt + 1) * P], qTa[:, 2 * h + dc, TP + n_ * LS:TP + (n_ + 1) * LS]) for dc in range(2)], [KTnB[s], qTaB], [scsB[s]])
                        kb.op("act", "activation", [scsB[s]], [ETsB], out=ETs[:, :, n_, :, :].rearrange("p m h t -> p m (h t)"), in_=scs[s][:, 0:64].rearrange("p (m x) -> p m x", m=2), func=AF.Exp, scale=SC)
                        for h in range(4):
                            for dc in range(2):
                                kb.mm_group(nums[:, 2 * h + dc, n_ * LS:(n_ + 1) * LS], [(Vn[s][:, mt, (2 * h + dc) * P:(2 * h + dc + 1) * P], ETs[:, mt, n_, h, :]) for mt in range(2)], [VnB[s], ETsB], [numsB])
                    kb.mm_group(dens[:], [(ones_bf, ETs[:, mt].rearrange("p n h t -> p (n h t)")) for mt in range(2)], [onesB, ETsB], [densB])
                    kb.op("dve", "reciprocal", [densB], [rdsB], out=rds.rearrange("p n h t -> p (n h t)"), in_=dens[:])
                    for h in range(4):
                        for dc in range(2):
                            kb.op("dve", "tensor_tensor", [numsB, rdsB], [aoB], out=ao[:, 2 * h + dc, TP:T].rearrange("p (n t) -> p n t", t=LS),
                                  in0=nums[:, 2 * h + dc, :].rearrange("p (n t) -> p n t", t=LS), in1=rds[:, :, h, :], op=ALU.mult)
                kb.soft_barrier()
                RA.p = mark0
                with contextlib.ExitStack() as s2:
                  if alvl >= 6:
                    pg = [ps(s2, "asg%d" % i, (P, 512), F32) for i in range(2)]
                    pgB = [Buf(), Buf()]
                    sl = [RA.alloc((P, 384), BF16) for i in range(2)]
                    slB = [Buf(), Buf()]
                    cnt = 0
                    for cb in range(2):
                        wg, wgB = wload_big(w_in_d[:, 8192 + cb * 512:8192 + (cb + 1) * 512], KC, 512)
                        for ci in range(4):
                            c = cb * 4 + ci
                            for tb in range(3):
                                s = cnt % 2
                                cnt += 1
                                c0 = tb * 384
                                kb.mm_group(pg[s][:, 0:384], [(wg[:, k, ci * P:(ci + 1) * P], hT[:, k, TAIL + c0:TAIL + c0 + 384]) for k in range(KC)], [hTB] + wgB, [pgB[s]])
                                kb.op("act", "activation", [pgB[s]], [slB[s]], out=sl[s], in_=pg[s][:, 0:384], func=AF.Silu)
                                kb.op("dve", "tensor_tensor", [aoB, slB[s]], [aoB], out=ao[:, c, c0:c0 + 384], in0=ao[:, c, c0:c0 + 384], in1=sl[s], op=ALU.mult)
                kb.soft_barrier()
            if "dbg" in phases:
                kb.dma("sp", kb.dsem(), dbg["ao"], ao, [aoB], [])

        if ALL or "merge" in phases:
            RM = Region(149 * KIB, 200 * KIB)
            RS2 = Region(14 * KIB, 26 * KIB)
            merged = RM.alloc((P, KC, T), BF16)
            mgB = Buf()
            with contextlib.ExitStack() as s2:
                acc = RM.alloc((P, 2, T), F32)
                accB = Buf()
                gt = [RS2.alloc((P, 384), F32) for i in range(2)]
                gtB = [Buf(), Buf()]
                tt = [RS2.alloc((P, 384), F32) for i in range(2)]
                ttB = [Buf(), Buf()]
                pgt = [ps(s2, "mpg%d" % i, (P, 512), F32) for i in range(3)]
                pgtB = [Buf() for _ in range(3)]
                ppp = [ps(s2, "mpp%d" % i, (P, 512), F32) for i in range(3)]
                pppB = [Buf() for _ in range(3)]
                srcs = [(a_, aB, w_co_d), (ob, obB, w_ho_d), (ao, aoB, w_ao_d)]
                cnt = 0
                for g in range(8):
                    for i in range(3):
                        bsrc, bsrcB, wdr = srcs[i]
                        wg, wgB = wload_sm(w_in_d[:, 9216 + i * 2048 + g * 256:9216 + i * 2048 + (g + 1) * 256], KC, 256)
                        wo, woB = wload_sm(wdr[:, g * 256:(g + 1) * 256], 8, 256)
                        for fi in range(2):
                            f = g * 2 + fi
                            for tb in range(3):
                                s = cnt % 3
                                s2_ = cnt % 2
                                cnt += 1
                                c0 = tb * 384
                                kb.mm_group(pgt[s][:, 0:384], [(wg[:, k, fi * P:(fi + 1) * P], hT[:, k, TAIL + c0:TAIL + c0 + 384]) for k in range(KC)], [hTB] + wgB, [pgtB[s]])
                                kb.mm_group(ppp[s][:, 0:384], [(wo[:, k, fi * P:(fi + 1) * P], bsrc[:, k, c0:c0 + 384]) for k in range(8)], [bsrcB] + woB, [pppB[s]])
                                kb.op("act", "activation", [pgtB[s], colsB], [gtB[s2_]], out=gt[s2_], in_=pgt[s][:, 0:384], func=AF.Sigmoid, bias=cols[:, C_BGATE + i * 16 + f:C_BGATE + i * 16 + f + 1])
                                if i == 0:
                                    kb.op("dve", "tensor_tensor", [pppB[s], gtB[s2_]], [accB], out=acc[:, fi, c0:c0 + 384], in0=ppp[s][:, 0:384], in1=gt[s2_], op=ALU.mult)
                                else:
                                    kb.op("dve", "tensor_tensor", [pppB[s], gtB[s2_]], [ttB[s2_]], out=tt[s2_], in0=ppp[s][:, 0:384], in1=gt[s2_], op=ALU.mult)
                                    if i == 1:
                                        kb.op("dve", "tensor_tensor", [ttB[s2_], accB], [accB], out=acc[:, fi, c0:c0 + 384], in0=acc[:, fi, c0:c0 + 384], in1=tt[s2_], op=ALU.add)
                                    else:
                                        kb.op("dve", "tensor_tensor", [ttB[s2_], accB], [mgB], out=merged[:, f, c0:c0 + 384], in0=acc[:, fi, c0:c0 + 384], in1=tt[s2_], op=ALU.add)
            kb.soft_barrier()
            if "dbg" in phases:
                kb.dma("sp", kb.dsem(), dbg["mg"], merged, [mgB], [])
            RO = Region(58 * KIB, 149 * KIB)
            with contextlib.ExitStack() as s2:
                xres = RO.alloc((P, 9, D), F32)
                gfin = RO.alloc((P, D), F32)
                xrB = [Buf() for _ in range(9)]
                xrS = [kb.dsem() for _ in range(9)]
                gfB = Buf()
                kb.dma("sp", kb.dsem(), gfin, gfin_d[:, :], [], [gfB])
                for tl in range(9):
                    kb.dma("sp", xrS[tl], xres[:, tl, :], x_d[tl * P:(tl + 1) * P, :], [], [xrB[tl]])
                py = [ps(s2, "ypy%d" % i, (P, 512), F32) for i in range(4)]
                pyB = [Buf() for _ in range(4)]
                junk2 = RS2.alloc((P, D), BF16)
                j2B = Buf()
                def final_norm(tl):
                        sB = Buf()
                        kb.op("act", "activation", [xrB[tl]], [j2B, sB], out=junk2, in_=xres[:, tl, :], func=AF.Square, accum_out=gstat[:, 3 * tl:3 * tl + 1])
                        kb.op("act", "activation", [sB], [sB], out=gstat[:, 3 * tl + 1:3 * tl + 2], in_=gstat[:, 3 * tl:3 * tl + 1], func=AF.Ln, scale=1.0 / D, bias=EPS)
                        kb.op("act", "activation", [sB], [sB], out=gstat[:, 3 * tl + 2:3 * tl + 3], in_=gstat[:, 3 * tl + 1:3 * tl + 2], func=AF.Exp, scale=-0.5)
                        kb.op("act", "activation", [xrB[tl], sB], [xrB[tl]], out=xres[:, tl, :], in_=xres[:, tl, :], func=AF.Copy, scale=gstat[:, 3 * tl + 2:3 * tl + 3])
                        kb.op("pool", "tensor_tensor", [xrB[tl], gfB], [xrB[tl]], out=xres[:, tl, :], in0=xres[:, tl, :], in1=gfin, op=ALU.mult)
                        kb.dma("sp", xrS[tl], y_d[tl * P:(tl + 1) * P, :], xres[:, tl, :], [xrB[tl]], [])


                cnt = 0
                for cb in range(4):
                    wo, woB = wload_big(w_out_d[:, cb * 512:(cb + 1) * 512], KC, 512)
                    for tl in range(9):
                        s = cnt % 4
                        cnt += 1
                        kb.mm_group(py[s][:], [(merged[:, k, tl * P:(tl + 1) * P], wo[:, k, :]) for k in range(KC)], [mgB] + woB, [pyB[s]])
                        kb.op("dve", "tensor_tensor", [pyB[s], xrB[tl]], [xrB[tl]], out=xres[:, tl, cb * 512:(cb + 1) * 512], in0=py[s][:], in1=xres[:, tl, cb * 512:(cb + 1) * 512], op=ALU.add)
                        if cb == 3:
                            final_norm(tl)
        assert not wcache, list(wcache)
        kb.barrier()
        with nc.Block() as block:
            @block.tensor
            def _(e):
                kb.replay(e, "pe")

            @block.scalar
            def _(e):
                kb.replay(e, "act")

            @block.vector
            def _(e):
                kb.replay(e, "dve")

            @block.gpsimd
            def _(e):
                kb.replay(e, "pool")

            @block.sync
            def _(e):
                kb.replay(e, "sp")
    return nc


def make_consts():
    c = np.zeros((P, NCONST), np.float32)
    c[:, K_IDENT:K_IDENT + P] = np.eye(P, dtype=np.float32)
    s = np.arange(P)[:, None]
    t = np.arange(P)[None, :]
    c[:, K_MPR:K_MPR + P] = ((s // 64 == t // 64) & (s <= t)).astype(np.float32)
    c[:, K_MSM:K_MSM + P] = ((s // LS == t // LS) & (s <= t)).astype(np.float32)
    c[:, K_SEL:K_SEL + NSEQ] = (s // LS == np.arange(NSEQ)[None, :]).astype(np.float32)
    r = np.ones(T, np.float32)
    r[0:TP:64] = 0.0
    r[TP::LS] = 0.0
    c[:, K_RST:K_RST + T] = r[None, :]
    return c


def colmaj(v):
    v = np.asarray(v, np.float32)
    return np.ascontiguousarray(v.reshape(-1, P).T)


def make_cols(inp):
    c = np.zeros((P, NCOLS), np.float32)
    c[:, C_GPRE:C_GPRE + 16] = colmaj(inp["g_pre"][0])
    c[:, C_GMEM:C_GMEM + 16] = colmaj(inp["g_mem"][0])
    c[:, C_CONVB:C_CONVB + 8] = colmaj(inp["conv_b"][0])
    c[:, C_LNG:C_LNG + 8] = colmaj(inp["ln_conv_g"][0])
    c[:, C_LNB:C_LNB + 8] = colmaj(inp["ln_conv_b"][0])
    c[:, C_LB0:C_LB0 + 8] = colmaj(inp["lb_logits"][0])
    c[:, C_LB1:C_LB1 + 8] = colmaj(inp["lb_logits"][1])
    c[:, C_GHG:C_GHG + 8] = colmaj(inp["g_hgrn_norm"][0])
    for i in range(3):
        c[:, C_BGATE + 16 * i:C_BGATE + 16 * (i + 1)] = colmaj(inp["b_gate"][0, i])
    for j in range(31):
        c[:, C_CONVW + 8 * j:C_CONVW + 8 * (j + 1)] = colmaj(inp["conv_w"][0, j])
    return c


def make_in_maps(inp):
    consts = make_consts()
    cols = make_cols(inp)
    gfin = np.ascontiguousarray(np.broadcast_to(np.asarray(inp["g_final"], np.float32)[None, :], (P, D)))
    xp = np.asarray(inp["x_prompt"], np.float32)
    xs = np.asarray(inp["x_sample"], np.float32)
    maps = []
    for c in range(NCORES):
        b, hf = c // 2, c % 2
        x = np.concatenate([xp[b, hf * TP:(hf + 1) * TP], xs[c * NSEQ:(c + 1) * NSEQ].reshape(TS, D)], axis=0)
        xprev = xp[b, 0:TP] if hf == 1 else np.zeros((TP, D), np.float32)
        maps.append({
            "x": np.ascontiguousarray(x),
            "xprev": np.ascontiguousarray(xprev),
            "mem": np.ascontiguousarray(inp["mem_prompt"][b]),
            "kc": np.ascontiguousarray(np.asarray(inp["cache_mem_k"])[0, c * NSEQ:(c + 1) * NSEQ].reshape(NSEQ, 256, 1024)),
            "vc": np.ascontiguousarray(np.asarray(inp["cache_mem_v"])[0, c * NSEQ:(c + 1) * NSEQ].reshape(NSEQ, 256, 1024)),
            "sconv": np.ascontiguousarray(np.asarray(inp["state_conv"])[0, c * NSEQ:(c + 1) * NSEQ]),
            "shgrn": np.ascontiguousarray(np.asarray(inp["state_hgrn"])[0, c * NSEQ:(c + 1) * NSEQ]),
            "w_in": np.ascontiguousarray(inp["w_in"][0]),
            "w_conv_out": np.ascontiguousarray(inp["w_conv_out"][0]),
            "w_hgrn_out": np.ascontiguousarray(inp["w_hgrn_out"][0]),
            "w_attn_out": np.ascontiguousarray(inp["w_attn_out"][0]),
            "w_mem_kv": np.ascontiguousarray(inp["w_mem_kv"][0]),
            "w_out": np.ascontiguousarray(inp["w_out"][0]),
            "cols": cols,
            "consts": consts,
            "gfin": gfin,
        })
    return maps


def assemble(res):
    y_p = np.zeros((4, 2048, D), np.float32)
    y_s = np.zeros((128, 8, D), np.float32)
    conv_p = np.zeros((1, 4, 30, 1024), np.float32)
    hg_p = np.zeros((1, 4, 8, P, P), np.float32)
    mk = np.zeros((1, 4, 256, 4, 256), np.float32)
    mv = np.zeros((1, 4, 256, 4, 256), np.float32)
    conv_s = np.zeros((1, 128, 30, 1024), np.float32)
    hg_s = np.zeros((1, 128, 8, P, P), np.float32)
    for c in range(NCORES):
        r = res[c]
        b, hf = c // 2, c % 2
        y_p[b, hf * TP:(hf + 1) * TP] = r["y"][0:TP]
        y_s[c * NSEQ:(c + 1) * NSEQ] = r["y"][TP:].reshape(NSEQ, LS, D)
        conv_s[0, c * NSEQ:(c + 1) * NSEQ] = r["conv_s"]
        hg_s[0, c * NSEQ:(c + 1) * NSEQ] = r["hgrn_s"]
        if hf == 1:
            conv_p[0, b] = r["conv_p"][2:32]
            hg_p[0, b] = r["hgrn_p"]
        else:
            mk[0, b] = r["mk"].reshape(256, 4, 256)
            mv[0, b] = r["mv"].reshape(256, 4, 256)
    return (y_p, y_s, conv_p, hg_p, mk, mv, conv_s, hg_s)


_CACHE = {}


def kernel(**inputs):
    inp = {k: np.asarray(v) for k, v in inputs.items()}
    if "nc" not in _CACHE:
        _CACHE["nc"] = build_program()
    nc = _CACHE["nc"]
    maps = make_in_maps(inp)
    res = run_bass_kernel_spmd(nc, maps, core_ids=list(range(NCORES)))
    return assemble(res.results)
```

```python
import numpy as np
import contextlib
import concourse.bass as bass
import concourse.mybir as mybir
from concourse.bass_utils import run_bass_kernel_spmd

F32 = mybir.dt.float32
BF16 = mybir.dt.bfloat16
AF = mybir.ActivationFunctionType
ALU = mybir.AluOpType
AX = mybir.AxisListType

P = 128
D = 2048
DIN = 15360
TP = 1024
TS = 128
T = TP + TS
TAIL = 32
TT = TAIL + T
NSEQ = 16
LS = 8
EPS = 1e-6
NCORES = 8
KC = D // P

C_GPRE, C_GMEM, C_CONVB, C_LNG, C_LNB, C_LB0, C_LB1, C_GHG, C_BGATE, C_CONVW = 0, 16, 32, 40, 48, 56, 64, 72, 80, 128
NCOLS = 128 + 31 * 8
K_IDENT, K_MPR, K_MSM, K_SEL, K_RST = 0, 128, 256, 384, 400
NCONST = 400 + T


class Tok:
    __slots__ = ("sem", "val", "key")

    def __init__(self, sem, val, key):
        self.sem, self.val, self.key = sem, val, key


_EPOCH = [0]
_SNAP = {}


class Buf:
    __slots__ = ("name", "w", "r", "epoch", "fenced")

    def __init__(self, name=""):
        self.name = name
        self.w = None
        self.r = []
        self.epoch = _EPOCH[0]
        self.fenced = False


class DSem:
    def __init__(self, h, key):
        self.h, self.key, self.count = h, key, 0


class EngRec:
    def __init__(self, name, sem, key):
        self.name, self.sem, self.key = name, sem, key
        self.count = 0
        self.ops = []
        self.waited = {}


class KB:
    def __init__(self, nc):
        _EPOCH[0] = 0
        _SNAP.clear()
        self.nc = nc
        self.nsem = 0
        self.E = {}
        for n in ("pe", "act", "dve", "pool", "sp"):
            self.E[n] = EngRec(n, self.new_sem("e_" + n), n)
        self.dsems = []

    def new_sem(self, name):
        self.nsem += 1
        return self.nc.alloc_semaphore(name=name + str(self.nsem))

    def dsem(self):
        d = DSem(self.new_sem("d"), "d%d" % len(self.dsems))
        self.dsems.append(d)
        return d

    def _waits(self, e, reads, writes, extra=()):
        E = self.E[e]
        deps = []
        for b in reads:
            if b.w is not None:
                deps.append(b.w)
        for b in writes:
            if b.w is not None:
                deps.append(b.w)
            for t in b.r:
                deps.append(t)
        deps.extend(extra)
        for b in list(reads) + list(writes):
            if not b.fenced:
                b.fenced = True
                deps.extend(_SNAP.get(b.epoch, ()))
        if e == "pe":
            deps = [t for t in deps if t.key != "pe"]
        ws = []
        for t in deps:
            if E.waited.get(t.key, 0) < t.val:
                E.waited[t.key] = t.val
                ws.append((t.sem, t.val))
        best = {}
        for s, v in ws:
            k = id(s)
            if k not in best or best[k][1] < v:
                best[k] = (s, v)
        return list(best.values())

    def _commit(self, tok, reads, writes):
        for b in reads:
            b.r.append(tok)
        for b in writes:
            b.w = tok
            b.r = []

    def op(self, e, fn, reads, writes, signal=True, **kw):
        E = self.E[e]
        ws = self._waits(e, reads, writes)
        if signal:
            E.count += 1
            tok = Tok(E.sem, E.count, e)
            E.ops.append((ws, fn, kw, (E.sem, 1)))
            self._commit(tok, reads, writes)
            return tok
        E.ops.append((ws, fn, kw, None))
        return None

    def dma(self, q, ds, out, in_, reads, writes, **kw):
        E = self.E[q]
        ws = self._waits(q, reads, writes)
        ds.count += 16
        tok = Tok(ds.h, ds.count, ds.key)
        kw = dict(kw)
        kw.update(out=out, in_=in_)
        E.ops.append((ws, "dma_start", kw, (ds.h, 16)))
        self._commit(tok, reads, writes)
        return tok

    def mm_group(self, out, pairs, reads, writes):
        E = self.E["pe"]
        ws = self._waits("pe", reads, writes)
        n = len(pairs)
        for i, pr in enumerate(pairs):
            if len(pr) == 3:
                o, l, r = pr
            else:
                o = out
                l, r = pr
            kw = dict(out=o, lhsT=l, rhs=r, start=(i == 0), stop=(i == n - 1))
            last = i == n - 1
            if last:
                E.count += 1
            E.ops.append((ws if i == 0 else [], "matmul", kw, (E.sem, 1) if last else None))
        tok = Tok(E.sem, E.count, "pe")
        self._commit(tok, reads, writes)
        return tok

    def transposes(self, items, reads, writes):
        E = self.E["pe"]
        ws = self._waits("pe", reads, writes)
        n = len(items)
        for i, (o, a, idn) in enumerate(items):
            last = i == n - 1
            if last:
                E.count += 1
            E.ops.append((ws if i == 0 else [], "transpose", dict(out=o, in_=a, identity=idn), (E.sem, 1) if last else None))
        tok = Tok(E.sem, E.count, "pe")
        self._commit(tok, reads, writes)
        return tok

    def barrier(self):
        toks = [Tok(E.sem, E.count, E.key) for E in self.E.values() if E.count > 0]
        toks += [Tok(d.h, d.count, d.key) for d in self.dsems if d.count > 0]
        for e, E in self.E.items():
            ws = []
            for t in toks:
                if t.key != e and E.waited.get(t.key, 0) < t.val:
                    E.waited[t.key] = t.val
                    ws.append((t.sem, t.val))
            if ws:
                E.ops.append((ws, None, None, None))

    def soft_barrier(self):
        toks = [Tok(E.sem, E.count, E.key) for E in self.E.values() if E.count > 0]
        toks += [Tok(d.h, d.count, d.key) for d in self.dsems if d.count > 0]
        _EPOCH[0] += 1
        _SNAP[_EPOCH[0]] = toks

    def pe_fence(self):
        E = self.E["pe"]
        ws = []
        for e2 in ("act", "dve", "pool"):
            E2 = self.E[e2]
            if E2.count > 0 and E.waited.get(e2, 0) < E2.count:
                E.waited[e2] = E2.count
                ws.append((E2.sem, E2.count))
        if ws:
            E.ops.append((ws, None, None, None))

    def replay(self, eng, e):
        for ws, fn, kw, inc in self.E[e].ops:
            for s, v in ws:
                eng.wait_ge(s, v)
            if fn is None:
                continue
            ins = getattr(eng, fn)(**kw)
            if inc is not None:
                ins.then_inc(inc[0], inc[1])


def build_program(phases=("all",)):
    nc = bass.Bass("TRN2", target_bir_lowering=False)
    kb = KB(nc)
    ALL = "all" in phases
    din = lambda n, s: nc.dram_tensor(n, list(s), F32, kind="ExternalInput").ap()
    dout = lambda n, s: nc.dram_tensor(n, list(s), F32, kind="ExternalOutput").ap()
    x_d = din("x", (T, D))
    xp_d = din("xprev", (TP, D))
    mem_d = din("mem", (256, D))
    kc_d = din("kc", (NSEQ, 256, 1024))
    vc_d = din("vc", (NSEQ, 256, 1024))
    sconv_d = din("sconv", (NSEQ, 30, 1024))
    shg_d = din("shgrn", (NSEQ, 8, P, P))
    w_in_d = din("w_in", (D, DIN))
    w_co_d = din("w_conv_out", (1024, D))
    w_ho_d = din("w_hgrn_out", (1024, D))
    w_ao_d = din("w_attn_out", (1024, D))
    w_kv_d = din("w_mem_kv", (D, D))
    w_out_d = din("w_out", (D, D))
    cols_d = din("cols", (P, NCOLS))
    consts_d = din("consts", (P, NCONST))
    gfin_d = din("gfin", (P, D))
    y_d = dout("y", (T, D))
    convp_d = dout("conv_p", (32, 1024))
    hgp_d = dout("hgrn_p", (8, P, P))
    mk_d = dout("mk", (256, 1024))
    mv_d = dout("mv", (256, 1024))
    convs_d = dout("conv_s", (NSEQ, 30, 1024))
    hgs_d = dout("hgrn_s", (NSEQ, 8, P, P))
    dbg = {}
    if "dbg" in phases:
        dbg["a"] = nc.dram_tensor("dbg_a", [P, 8, T], BF16, kind="ExternalOutput").ap()
        dbg["ob"] = nc.dram_tensor("dbg_ob", [P, 8, T], BF16, kind="ExternalOutput").ap()
        dbg["ao"] = nc.dram_tensor("dbg_ao", [P, 8, T], BF16, kind="ExternalOutput").ap()
        dbg["mg"] = nc.dram_tensor("dbg_mg", [P, 16, T], BF16, kind="ExternalOutput").ap()

    KIB = 1024
    ARENA = 200 * KIB
    ES = contextlib.ExitStack()
    uniq = {"n": 0}

    def ps(st, name, shape, dt):
        uniq["n"] += 1
        kb.pe_fence()
        return st.enter_context(nc.psum_tensor("p%d_%s" % (uniq["n"], name), list(shape), dt))

    with ES:
        arena = ES.enter_context(nc.sbuf_tensor("arena", [P, ARENA // 4], F32))

        class Region:
            def __init__(self, lo, hi):
                self.lo, self.hi, self.p = lo, hi, lo

            def reset(self):
                self.p = self.lo

            def alloc(self, shape, dt):
                es = 2 if dt == BF16 else 4
                n = 1
                for d in shape[1:]:
                    n *= d
                nb = (n * es + 31) // 32 * 32
                off = self.p
                self.p += nb
                assert self.p <= self.hi, ("region overflow", shape, self.p, self.hi)
                ap = arena[0:shape[0], off // 4:(off + nb) // 4]
                if dt == BF16:
                    ap = ap.bitcast(BF16)
                ap = ap[:, 0:n]
                if len(shape) == 3:
                    ap = ap.rearrange("p (a b) -> p a b", a=shape[1])
                elif len(shape) == 4:
                    ap = ap.rearrange("p (a b c) -> p a b c", a=shape[1], b=shape[2])
                elif len(shape) == 5:
                    ap = ap.rearrange("p (a b c d) -> p a b c d", a=shape[1], b=shape[2], c=shape[3])
                return ap

        R_G0 = Region(0, 14 * KIB)
        R_S = Region(14 * KIB, 26 * KIB)
        R_W = Region(26 * KIB, 58 * KIB)
        R_HT = Region(58 * KIB, 95 * KIB)
        R_OB = Region(95 * KIB, 113 * KIB)
        R_A = Region(113 * KIB, 131 * KIB)
        R_AO = Region(131 * KIB, 149 * KIB)

        cols = R_G0.alloc((P, NCOLS), F32)
        consts = R_G0.alloc((P, NCONST), F32)
        ident_bf = R_G0.alloc((P, P), BF16)
        ones_bf = R_G0.alloc((P, P), BF16)
        lbc = R_G0.alloc((P, 32), F32)
        tailT = R_G0.alloc((P, KC, TAIL), BF16)
        gstat = R_G0.alloc((P, 96), F32)
        hT = R_HT.alloc((P, KC, TT), BF16)
        ob = R_OB.alloc((P, 8, T), BF16)
        a_ = R_A.alloc((P, 8, T), BF16)
        ao = R_AO.alloc((P, 8, T), BF16)
        wsm = [R_W.alloc((P, 4096), BF16) for i in range(4)]
        wsmB = [Buf("w%d" % i) for i in range(4)]
        wsmS = [kb.dsem() for i in range(4)]
        wbig = [arena[:, (26 * KIB + i * 16 * KIB) // 4:(26 * KIB + (i + 1) * 16 * KIB) // 4].bitcast(BF16) for i in range(2)]
        wstate = {"p": 0}
        colsB, constsB, identB, onesB, lbcB, hTB, tailB = Buf(), Buf(), Buf(), Buf(), Buf(), Buf(), Buf()
        obB, aB, aoB = Buf(), Buf(), Buf()
        ident_f = consts[:, K_IDENT:K_IDENT + P]
        mask_pr = consts[:, K_MPR:K_MPR + P]
        mask_sm = consts[:, K_MSM:K_MSM + P]

        ds0 = kb.dsem()
        kb.dma("sp", ds0, cols, cols_d[:, :], [], [colsB])
        kb.dma("sp", ds0, consts, consts_d[:, :], [], [constsB])
        kb.op("dve", "tensor_copy", [constsB], [identB], out=ident_bf, in_=ident_f)
        kb.op("dve", "memset", [], [onesB], ap=ones_bf, constant=1.0)
        kb.op("dve", "tensor_tensor", [colsB], [lbcB], out=lbc[:, 0:8], in0=cols[:, C_LB1:C_LB1 + 8], in1=cols[:, C_LB0:C_LB0 + 8], op=ALU.subtract)
        kb.op("act", "activation", [lbcB], [lbcB], out=lbc[:, 8:16], in_=lbc[:, 0:8], func=AF.Sigmoid)
        kb.op("dve", "tensor_scalar", [lbcB], [lbcB], out=lbc[:, 16:24], in0=lbc[:, 8:16], scalar1=-1.0, scalar2=None, op0=ALU.mult)

        wcache = {}

        def wload_big(dram2d, kc, n, key=None):
            if key is not None and key in wcache:
                return wcache.pop(key)
            p_ = wstate["p"]
            if p_ % 2:
                p_ += 1
            i = (p_ // 2) % 2
            wstate["p"] = (p_ + 2) % 4
            view = wbig[i][:, 0:kc * n].rearrange("p (k n) -> p k n", k=kc)
            bs = [wsmB[2 * i], wsmB[2 * i + 1]]
            kb.dma("pool", wsmS[2 * i], view, dram2d.rearrange("(k p) n -> p k n", p=P), [], bs)
            return view, bs

        def wload_sm(dram2d, kc, n, key=None):
            if key is not None and key in wcache:
                return wcache.pop(key)
            i = wstate["p"] % 4
            wstate["p"] = (i + 1) % 4
            view = wsm[i][:, 0:kc * n].rearrange("p (k n) -> p k n", k=kc)
            kb.dma("pool", wsmS[i], view, dram2d.rearrange("(k p) n -> p k n", p=P), [], [wsmB[i]])
            return view, [wsmB[i]]

        def norm_phase(R, st, jobs, tag):
            xt = [R.alloc((P, D), F32) for i in range(2)]
            xtB = [Buf() for _ in range(2)]
            xtS = [kb.dsem() for _ in range(2)]
            junk = R.alloc((P, D), BF16)
            junkB = Buf()
            xs = [R.alloc((P, D), BF16) for i in range(2)]
            xsB = [Buf() for _ in range(2)]
            stat = R.alloc((P, 3 * len(jobs)), F32)
            tp = [ps(st, tag + "tp%d" % i, (P, KC, P), BF16) for i in range(2)]
            tpB = [Buf() for _ in range(2)]
            for j, (rows, gc, dst, dstB) in enumerate(jobs):
                s = j % 2
                kb.dma("sp", xtS[s], xt[s], rows, [], [xtB[s]])
                sB = Buf()
                kb.op("act", "activation", [xtB[s]], [junkB, sB], out=junk, in_=xt[s], func=AF.Square, accum_out=stat[:, 3 * j:3 * j + 1])
                kb.op("act", "activation", [sB], [sB], out=stat[:, 3 * j + 1:3 * j + 2], in_=stat[:, 3 * j:3 * j + 1], func=AF.Ln, scale=1.0 / D, bias=EPS)
                kb.op("act", "activation", [sB], [sB], out=stat[:, 3 * j + 2:3 * j + 3], in_=stat[:, 3 * j + 1:3 * j + 2], func=AF.Exp, scale=-0.5)
                kb.op("dve", "tensor_scalar", [xtB[s], sB], [xsB[s]], out=xs[s], in0=xt[s], scalar1=stat[:, 3 * j + 2:3 * j + 3], scalar2=None, op0=ALU.mult)
                kb.transposes([(tp[s][:, k, :], xs[s][:, k * P:(k + 1) * P], ident_bf) for k in range(KC)], [xsB[s], identB], [tpB[s]])
                kb.op("dve", "tensor_tensor", [tpB[s], colsB], [dstB], out=dst, in0=tp[s][:], in1=cols[:, gc:gc + KC].unsqueeze(2).to_broadcast([P, KC, P]), op=ALU.mult)

        Sf = [R_S.alloc((P, 8, P), F32) for i in range(2)]
        Sb = [R_S.alloc((P, 8, P), BF16) for i in range(2)]
        SfB = [[Buf() for h in range(8)] for i in range(2)]
        SbB = [[Buf() for h in range(8)] for i in range(2)]
        for h in range(8):
            kb.op("pool", "memset", [], [SfB[0][h]], ap=Sf[0][:, h, :], constant=0.0)
            kb.op("pool", "memset", [], [SbB[0][h]], ap=Sb[0][:, h, :], constant=0.0)
        cur_h = [0] * 8

        def hgrn_v(st, hsrc, hsrcB, col0, ntile, vdst, vB, blk, tag):
            with contextlib.ExitStack() as s2:
                pv = [ps(s2, tag + "pv%d" % i, (P, 512), F32) for i in range(2)]
                pvB = [Buf() for _ in range(2)]
                wv, wB = wload_big(w_in_d[:, 5120 + blk * 512:5120 + (blk + 1) * 512], KC, 512)
                for tl in range(ntile):
                    s = tl % 2
                    kb.mm_group(pv[s][:], [(hsrc[:, k, col0 + tl * P:col0 + (tl + 1) * P], wv[:, k, :]) for k in range(KC)], [hsrcB] + wB, [pvB[s]])
                    kb.op("act", "activation", [pvB[s]], [vB], out=vdst[:, tl, :], in_=pv[s][:], func=AF.Copy)

        def hgrn_kq(R, hsrc, hsrcB, col0, blocks, want_q, qT, qB, kT, kB, kh, khB, ebC, ebCB, ebmap, hb, bw, tag, vspec=None, vrate=1):
            with contextlib.ExitStack() as s2:
                NSL = 2 if want_q else 3
                vjobs = []
                if vspec is not None:
                    ntile_v, vdst, vB_ = vspec
                    pv = [ps(s2, tag + "pv%d" % i, (P, 512), F32) for i in range(2)]
                    pvB = [Buf() for _ in range(2)]
                    wv, wvB = wload_big(w_in_d[:, 5120 + hb * 512:5120 + (hb + 1) * 512], KC, 512, key=tag + "hi%d" % hb)

                    def mk_v(tl):
                        def f_():
                            sv = tl % 2
                            kb.mm_group(pv[sv][:], [(hsrc[:, k, col0 + tl * P:col0 + (tl + 1) * P], wv[:, k, :]) for k in range(KC)], [hsrcB] + wvB, [pvB[sv]])
                            kb.op("dve", "tensor_copy", [pvB[sv]], [vB_], out=vdst[:, tl, :], in_=pv[sv][:])
                        return f_
                    vjobs = [mk_v(tl) for tl in range(ntile_v)]
                pf = [ps(s2, tag + "pf%d" % i, (P, 512), F32) for i in range(NSL)]
                pfB = [Buf() for _ in range(NSL)]
                pq = [ps(s2, tag + "pq%d" % i, (P, 512), F32) for i in range(NSL)] if want_q else []
                pqB = [Buf() for _ in range(NSL)]
                pt = [ps(s2, tag + "pt%d" % i, (P, 4, P), BF16) for i in range(2)]
                ptB = [Buf() for _ in range(2)]
                tmp = [[R.alloc((P, bw), F32) for j in range(5)] for i in range(NSL)]
                tmpB = [[Buf() for j in range(5)] for i in range(NSL)]
                KS = 4
                kht = [R.alloc((P, bw), BF16) for i in range(KS)]
                khtB = [Buf() for _ in range(KS)]
                cnt = 0
                pend_t = []
                if not want_q:
                    wf_big, wfB_big = wload_big(w_in_d[:, 4096 + hb * 512:4096 + (hb + 1) * 512], KC, 512, key=tag + "hf%d" % hb)
                for hh in range(4):
                    h = hb * 4 + hh
                    if want_q:
                        if hh % 2 == 0:
                            cf = 4096 + hb * 512 + (hh // 2) * 256
                            cq = 3072 + hb * 512 + (hh // 2) * 256
                            wf_sm, wfB = wload_sm(w_in_d[:, cf:cf + 256], KC, 256, key=tag + "hf%d_%d" % (hb, hh // 2))
                            wq_sm, wqB = wload_sm(w_in_d[:, cq:cq + 256], KC, 256, key=tag + "hq%d_%d" % (hb, hh // 2))
                        wf = wf_sm[:, :, (hh % 2) * P:(hh % 2 + 1) * P]
                        wq = wq_sm[:, :, (hh % 2) * P:(hh % 2 + 1) * P]
                    else:
                        wf = wf_big[:, :, hh * P:(hh + 1) * P]
                        wfB = wfB_big
                    for (c0, n, segs) in blocks:
                        s = cnt % NSL
                        cnt += 1
                        sk = (cnt - 1) % KS
                        sgn, ff, bb, eb, enb = [tmp[s][j][:, 0:n] for j in range(5)]
                        sgnB, ffB, bbB, ebB, enbB = tmpB[s]
                        kb.mm_group(pf[s][:, 0:n], [(wf[:, k, :], hsrc[:, k, col0 + c0:col0 + c0 + n]) for k in range(KC)], [hsrcB] + wfB, [pfB[s]])
                        if want_q:
                            kb.mm_group(pq[s][:, 0:n], [(wq[:, k, :], hsrc[:, k, col0 + c0:col0 + c0 + n]) for k in range(KC)], [hsrcB] + wqB, [pqB[s]])
                        while len(pend_t) > 1:
                            pend_t.pop(0)()
                        kb.op("act", "activation", [pfB[s]], [sgnB], out=sgn, in_=pf[s][:, 0:n], func=AF.Sigmoid, scale=-1.0)
                        kb.op("dve", "tensor_scalar", [sgnB, lbcB], [ffB], out=ff, in0=sgn, scalar1=lbc[:, 16 + h:17 + h], scalar2=1.0, op0=ALU.mult, op1=ALU.add)
                        kb.op("act", "activation", [ffB], [ffB], out=ff, in_=ff, func=AF.Ln)
                        kb.op("dve", "tensor_tensor_scan", [ffB, constsB], [bbB], out=bb, data0=consts[:, K_RST + c0:K_RST + c0 + n], data1=ff, initial=0.0, op0=ALU.mult, op1=ALU.add)
                        kb.op("act", "activation", [bbB], [ebB], out=eb, in_=bb, func=AF.Exp)
                        kb.op("act", "activation", [bbB], [enbB], out=enb, in_=bb, func=AF.Exp, scale=-1.0)
                        if want_q:
                            kb.op("dve", "scalar_tensor_tensor", [pqB[s], ebB], [qB], out=qT[:, hh, c0:c0 + n], in0=pq[s][:, 0:n], scalar=float(P) ** -0.5, in1=eb, op0=ALU.mult, op1=ALU.mult)
                        kb.op("dve", "scalar_tensor_tensor", [sgnB, enbB, lbcB], [kB], out=kT[:, hh, c0:c0 + n], in0=sgn, scalar=lbc[:, 8 + h:9 + h], in1=enb, op0=ALU.mult, op1=ALU.mult)
                        for (off, ln, L) in segs:
                            nch = ln // L
                            ch0 = ebmap[c0 + off]
                            ebv = eb[:, off:off + ln].rearrange("p (c l) -> p c l", l=L)[:, :, L - 1:L]
                            kb.op("act", "activation", [ebB], [ebCB], out=ebC[:, hh, ch0:ch0 + nch].unsqueeze(2), in_=ebv, func=AF.Copy)
                            kb.op("dve", "tensor_tensor", [kB, ebB], [khtB[sk]], out=kht[sk][:, off:off + ln].rearrange("p (c l) -> p c l", l=L),
                                  in0=kT[:, hh, c0 + off:c0 + off + ln].rearrange("p (c l) -> p c l", l=L), in1=ebv.to_broadcast([P, nch, L]), op=ALU.mult)
                        def mk_t(s=sk, sp_=cnt % 2, n=n, c0=c0, hh=hh):
                            def f_():
                                ntl = n // P
                                kb.transposes([(pt[sp_][:, i, :], kht[s][:, i * P:(i + 1) * P], ident_bf) for i in range(ntl)], [khtB[s], identB], [ptB[sp_]])
                                t0 = c0 // P
                                kb.op("act", "activation", [ptB[sp_]], [khB], out=kh[:, t0:t0 + ntl, hh * P:(hh + 1) * P], in_=pt[sp_][:, 0:ntl, :], func=AF.Copy)
                            return f_
                        pend_t.append(mk_t())
                        for _ in range(vrate):
                            if vjobs:
                                vjobs.pop(0)()
                while pend_t:
                    pend_t.pop(0)()
                while vjobs:
                    vjobs.pop(0)()

        def s_update(U_ap, UB, h, ebc_ap, ebcB):
            cur = cur_h[h]
            nxt = 1 - cur
            kb.op("dve", "scalar_tensor_tensor", [SfB[cur][h], UB, ebcB], [SfB[nxt][h]], out=Sf[nxt][:, h, :], in0=Sf[cur][:, h, :], scalar=ebc_ap, in1=U_ap, op0=ALU.mult, op1=ALU.add)
            kb.op("act", "activation", [SfB[nxt][h]], [SbB[nxt][h]], out=Sb[nxt][:, h, :], in_=Sf[nxt][:, h, :], func=AF.Copy)
            cur_h[h] = nxt

        RX = Region(58 * KIB, 200 * KIB)
        with contextlib.ExitStack() as st:
            hTp = RX.alloc((P, KC, TP), BF16)
            hTpB = Buf()
            mark = RX.p
            with contextlib.ExitStack() as s1:
                jobs = [(xp_d[i * P:(i + 1) * P, :], C_GPRE, hTp[:, :, i * P:(i + 1) * P], hTpB) for i in range(TP // P)]
                norm_phase(RX, s1, jobs, "np")
            kb.soft_barrier()
            RX.p = mark
            kb.op("dve", "tensor_copy", [hTpB], [tailB], out=tailT, in_=hTp[:, :, TP - TAIL:TP])
            ebmap_p = {c * 64: c for c in range(16)}
            blocks_p = [(0, 512, [(0, 512, 64)]), (512, 512, [(0, 512, 64)])]
            for hb in range(2):
                RX.p = mark
                vp = RX.alloc((P, 8, 512), BF16)
                khp = RX.alloc((P, 8, 512), BF16)
                kTp = RX.alloc((P, 4, TP), BF16)
                ebCp = RX.alloc((P, 4, 16), F32)
                vpB, khpB, kTpB, ebCpB = Buf(), Buf(), Buf(), Buf()
                hgrn_kq(RX, hTp, hTpB, 0, blocks_p, False, None, None, kTp, kTpB, khp, khpB, ebCp, ebCpB, ebmap_p, hb, 512, "p", vspec=(8, vp, vpB), vrate=1)
                if hb == 0:
                    wcache["phi1"] = wload_big(w_in_d[:, 5120 + 512:5120 + 1024], KC, 512)
                    wcache["phf1"] = wload_big(w_in_d[:, 4096 + 512:4096 + 1024], KC, 512)
                else:
                    wcache["mhi0"] = wload_big(w_in_d[:, 5120:5120 + 512], KC, 512)
                    wcache["mhf0_0"] = wload_sm(w_in_d[:, 4096:4096 + 256], KC, 256)
                    wcache["mhq0_0"] = wload_sm(w_in_d[:, 3072:3072 + 256], KC, 256)
                with contextlib.ExitStack() as s2:
                    pu = [ps(s2, "ppu%d" % i, (P, 4, P), F32) for i in range(4)]
                    puB = [Buf() for i in range(4)]
                    cnt = 0
                    for tl in range(8):
                        for c in range(2):
                            s = cnt % 4
                            cnt += 1
                            for hh in range(4):
                                kb.mm_group(pu[s][:, hh, :], [(khp[c * 64:(c + 1) * 64, tl, hh * P:(hh + 1) * P], vp[c * 64:(c + 1) * 64, tl, hh * P:(hh + 1) * P])], [khpB, vpB], [puB[s]])
                            for hh in range(4):
                                ch = tl * 2 + c
                                s_update(pu[s][:, hh, :], puB[s], hb * 4 + hh, ebCp[:, hh, ch:ch + 1], ebCpB)
                kb.barrier()

        RF = Region(149 * KIB, 200 * KIB)
        with contextlib.ExitStack() as st:
            kb.op("dve", "tensor_copy", [tailB], [hTB], out=hT[:, :, 0:TAIL], in_=tailT)
            jobs = [(x_d[i * P:(i + 1) * P, :], C_GPRE, hT[:, :, TAIL + i * P:TAIL + (i + 1) * P], hTB) for i in range(T // P)]
            norm_phase(RF, st, jobs, "nm")
        kb.soft_barrier()

        RH = Region(113 * KIB, 200 * KIB)
        ebmap_m = {c * 64: c for c in range(16)}
        for n_ in range(NSEQ):
            ebmap_m[TP + n_ * LS] = 16 + n_
        blocks_m = [(0, 384, [(0, 384, 64)]), (384, 384, [(0, 384, 64)]), (768, 384, [(0, 256, 64), (256, 128, LS)])]
        hgsS = kb.dsem()
        for hb in range(2):
            with contextlib.ExitStack() as st:
                RH.reset()
                v_h = RH.alloc((P, 9, 512), BF16)
                kh_h = RH.alloc((P, 9, 512), BF16)
                qT_h = RH.alloc((P, 4, T), BF16)
                kT_h = RH.alloc((P, 4, T), BF16)
                ebC_h = RH.alloc((P, 4, 32), F32)
                slh = RH.alloc((P, 4, T), BF16)
                slhB = Buf()
                vB, khB, qB, kB_, ebCB = Buf(), Buf(), Buf(), Buf(), Buf()
                mark = RH.p
                hgrn_kq(RH, hT, hTB, TAIL, blocks_m, True, qT_h, qB, kT_h, kB_, kh_h, khB, ebC_h, ebCB, ebmap_m, hb, 384, "m", vspec=(9, v_h, vB), vrate=2)
                kb.soft_barrier()
                RH.p = mark
                with contextlib.ExitStack() as s2:
                    pat = ps(s2, "pat", (P, 4, P), F32)
                    pu = [ps(s2, "pu%d" % i, (P, 4, P), F32) for i in range(2)]
                    po2 = [ps(s2, "po%d" % i, (P, 4, P), F32) for i in range(1)] * 2
                    pg = [ps(s2, "hpg%d" % i, (P, 512), F32) for i in range(2)]
                    pgB = [Buf(), Buf()]
                    pss = ps(s2, "pss", (P, 512), F32)
                    pos = ps(s2, "pos", (P, 4, P), F32)
                    patB, pssB, posB = Buf(), Buf(), Buf()
                    po2B = [Buf()] * 2
                    pend_r = []
                    wg, wgB = wload_big(w_in_d[:, 6144 + hb * 512:6144 + (hb + 1) * 512], KC, 512)

                    def mk_hog(hh, tb, idx):
                        def f_():
                            sg_ = idx % 2
                            c0 = tb * 384
                            kb.mm_group(pg[sg_][:, 0:384], [(wg[:, k, hh * P:(hh + 1) * P], hT[:, k, TAIL + c0:TAIL + c0 + 384]) for k in range(KC)], [hTB] + wgB, [pgB[sg_]])
                            kb.op("act", "activation", [pgB[sg_]], [slhB], out=slh[:, hh, c0:c0 + 384], in_=pg[sg_][:, 0:384], func=AF.Silu)
                        return f_
                    hogjobs = [mk_hog(hh, tb, tb * 4 + hh) for tb in range(3) for hh in range(4)]
                    puB = [Buf(), Buf()]
                    AT = [RH.alloc((P, 4, P), BF16) for i in range(2)]
                    ATB = [Buf(), Buf()]
                    sq = RH.alloc((P, 512), BF16)
                    sqB = Buf()
                    rr = RH.alloc((P, 512), F32)
                    rrB = Buf()
                    osb = RH.alloc((P, 4, P), F32)
                    osbB = Buf()
                    NSS = 4
                    S0f = [RH.alloc((P, 4, P), F32) for i in range(NSS)]
                    S0b = [RH.alloc((P, 4, P), BF16) for i in range(NSS)]
                    So = [RH.alloc((P, 4, P), F32) for i in range(NSS)]
                    vmk = [RH.alloc((P, 512), BF16) for i in range(NSS)]
                    S0fB, S0bB, SoB, vmkB = [[Buf() for _ in range(NSS)] for _ in range(4)]
                    S0fS, S0bS, SoS = [[kb.dsem() for _ in range(NSS)] for _ in range(3)]

                    def rmsnorm_o(o_ap, oB, tl, defer=False):
                        kb.op("act", "activation", [oB], [sqB], out=sq, in_=o_ap.rearrange("p a b -> p (a b)"), func=AF.Square)

                        def tail():
                            kb.mm_group(pss[:], [(ones_bf, sq)], [onesB, sqB], [pssB])
                            kb.op("act", "activation", [pssB], [rrB], out=rr, in_=pss[:], func=AF.Ln, scale=1.0 / P, bias=EPS)
                            kb.op("act", "activation", [rrB], [rrB], out=rr, in_=rr, func=AF.Exp, scale=-0.5)
                            for hh in range(4):
                                h = hb * 4 + hh
                                kb.op("dve", "scalar_tensor_tensor", [oB, rrB, colsB], [obB], out=ob[:, h, tl * P:(tl + 1) * P], in0=o_ap[:, hh, :], scalar=cols[:, C_GHG + h:C_GHG + h + 1], in1=rr[:, hh * P:(hh + 1) * P], op0=ALU.mult, op1=ALU.mult)
                        if defer:
                            pend_r.append(tail)
                        else:
                            tail()

                    for tl in range(9):
                        sa = tl % 2
                        po = po2[tl % 2]
                        poB = po2B[tl % 2]
                        msk = mask_pr if tl < 8 else mask_sm
                        for _ in range(2):
                            if hogjobs:
                                hogjobs.pop(0)()
                        for hh in range(4):
                            kb.mm_group(pat[:, hh, :], [(kT_h[:, hh, tl * P:(tl + 1) * P], qT_h[:, hh, tl * P:(tl + 1) * P])], [kB_, qB], [patB])
                        kb.op("dve", "tensor_tensor", [patB, constsB], [ATB[sa]], out=AT[sa], in0=pat[:], in1=msk.unsqueeze(1).to_broadcast([P, 4, P]), op=ALU.mult)
                        if tl < 8:
                            for c in range(2):
                                for hh in range(4):
                                    kb.mm_group(pu[c][:, hh, :], [(kh_h[c * 64:(c + 1) * 64, tl, hh * P:(hh + 1) * P], v_h[c * 64:(c + 1) * 64, tl, hh * P:(hh + 1) * P])], [khB, vB], [puB[c]])
                            before = [cur_h[hb * 4 + hh] for hh in range(4)]
                            for hh in range(4):
                                s_update(pu[0][:, hh, :], puB[0], hb * 4 + hh, ebC_h[:, hh, 2 * tl:2 * tl + 1], ebCB)
                            for hh in range(4):
                                h = hb * 4 + hh
                                b0 = before[hh]
                                b1 = 1 - b0
                                kb.mm_group(None, [
                                    (po[:, hh, :], v_h[:, tl, hh * P:(hh + 1) * P], AT[sa][:, hh, :]),
                                    (po[:, hh, 0:64], Sb[b0][:, h, :], qT_h[:, hh, tl * P:tl * P + 64]),
                                    (po[:, hh, 64:128], Sb[b1][:, h, :], qT_h[:, hh, tl * P + 64:(tl + 1) * P]),
                                ], [vB, ATB[sa], SbB[b0][h], SbB[b1][h], qB], [poB])
                            while pend_r:
                                pend_r.pop(0)()
                            for hh in range(4):
                                s_update(pu[1][:, hh, :], puB[1], hb * 4 + hh, ebC_h[:, hh, 2 * tl + 1:2 * tl + 2], ebCB)
                            rmsnorm_o(po[:], poB, tl)
                            if tl == 7:
                                for hh in range(4):
                                    h = hb * 4 + hh
                                    kb.dma("sp", hgsS, hgp_d[h], Sf[cur_h[h]][:, h, :], [SfB[cur_h[h]][h]], [])
                        else:
                            if hb == 0:
                                wcache["mhi1"] = wload_big(w_in_d[:, 5120 + 512:5120 + 1024], KC, 512)
                                wcache["mhf1_0"] = wload_sm(w_in_d[:, 4096 + 512:4096 + 768], KC, 256)
                                wcache["mhq1_0"] = wload_sm(w_in_d[:, 3072 + 512:3072 + 768], KC, 256)
                            else:
                                wcache["cv0"] = wload_sm(w_in_d[:, 0:256], KC, 256)
                                wcache["cg0"] = wload_sm(w_in_d[:, 1024:1280], KC, 256)
                            for hh in range(4):
                                kb.mm_group(po[:, hh, :], [(v_h[:, tl, hh * P:(hh + 1) * P], AT[sa][:, hh, :])], [vB, ATB[sa]], [poB])
                            while pend_r:
                                pend_r.pop(0)()
                            def ld_state(m_):
                                sm_ = m_ % NSS
                                src_ = shg_d[m_, hb * 4:(hb + 1) * 4].rearrange("h k v -> k h v")
                                kb.dma("sp", S0fS[sm_], S0f[sm_], src_, [], [S0fB[sm_]])
                                kb.dma("pool", S0bS[sm_], S0b[sm_], src_, [], [S0bB[sm_]])
                            for m_ in range(NSS - 1):
                                ld_state(m_)
                            for n_ in range(NSEQ):
                                s = n_ % NSS
                                sq_ = n_ % 2
                                if n_ + NSS - 1 < NSEQ:
                                    ld_state(n_ + NSS - 1)
                                for hh in range(4):
                                    kb.mm_group(pos[:, hh, n_ * LS:(n_ + 1) * LS], [(S0b[s][:, hh, :], qT_h[:, hh, TP + n_ * LS:TP + (n_ + 1) * LS])], [S0bB[s], qB], [posB])
                                kb.op("act", "activation", [vB, constsB], [vmkB[s]], out=vmk[s], in_=v_h[:, tl, :], func=AF.Copy, scale=consts[:, K_SEL + n_:K_SEL + n_ + 1])
                                for hh in range(4):
                                    kb.mm_group(pu[sq_][:, hh, :], [(kh_h[:, tl, hh * P:(hh + 1) * P], vmk[s][:, hh * P:(hh + 1) * P])], [khB, vmkB[s]], [puB[sq_]])
                                for hh in range(4):
                                    kb.op("dve", "scalar_tensor_tensor", [S0fB[s], puB[sq_], ebCB], [SoB[s]], out=So[s][:, hh, :], in0=S0f[s][:, hh, :], scalar=ebC_h[:, hh, 16 + n_:17 + n_], in1=pu[sq_][:, hh, :], op0=ALU.mult, op1=ALU.add)
                                kb.dma("sp", SoS[s], hgs_d[n_, hb * 4:(hb + 1) * 4].rearrange("h k v -> k h v"), So[s], [SoB[s]], [])
                            kb.op("act", "activation", [posB], [osbB], out=osb, in_=pos[:], func=AF.Copy)
                            kb.op("dve", "tensor_tensor", [poB, osbB], [osbB], out=osb, in0=po[:], in1=osb, op=ALU.add)
                            rmsnorm_o(osb, osbB, tl)
                    while hogjobs:
                        hogjobs.pop(0)()
                    for hh in range(4):
                        h = hb * 4 + hh
                        for tb in range(3):
                            c0 = tb * 384
                            kb.op("dve", "tensor_tensor", [obB, slhB], [obB], out=ob[:, h, c0:c0 + 384], in0=ob[:, h, c0:c0 + 384], in1=slh[:, hh, c0:c0 + 384], op=ALU.mult)
                kb.barrier()

        if "dbg" in phases:
            kb.dma("sp", kb.dsem(), dbg["ob"], ob, [obB], [])

        if ALL or "conv" in phases:
            RC = Region(131 * KIB, 200 * KIB)
            RS2 = Region(14 * KIB, 26 * KIB)
            with contextlib.ExitStack() as st:
                u_p = RC.alloc((P, 8, TAIL + TP), BF16)
                u_s = RC.alloc((P, 8, NSEQ, 38), BF16)
                ufp_s = RC.alloc((P, 8, P), F32)
                ufp_t = RC.alloc((P, 8, TAIL), F32)
                dcb = RC.alloc((P, 8, T), BF16)
                S1 = RS2.alloc((P, T), F32)
                S2 = RS2.alloc((P, T), F32)
                upB, usB, ufsB, uftB, dcbB, S1B, S2B = Buf(), Buf(), Buf(), Buf(), Buf(), Buf(), Buf()
                mark = RC.p
                with contextlib.ExitStack() as s2:
                    tl_in = [RC.alloc((120, 1024), F32) for i in range(2)]
                    tlB = [Buf(), Buf()]
                    tlS = [kb.dsem(), kb.dsem()]
                    ptl = [ps(s2, "ptl%d" % i, (P, 8, P), F32) for i in range(2)]
                    ptlB = [Buf(), Buf()]
                    for g in range(4):
                        s = g % 2
                        kb.dma("sp", tlS[s], tl_in[s], sconv_d[4 * g:4 * g + 4].rearrange("n r c -> (n r) c"), [], [tlB[s]])
                        kb.transposes([(ptl[s][:, c, 0:120], tl_in[s][:, c * P:(c + 1) * P], ident_f[0:120, 0:120]) for c in range(8)], [tlB[s], constsB], [ptlB[s]])
                        for c in range(8):
                            kb.op("act", "activation", [ptlB[s]], [usB], out=u_s[:, c, 4 * g:4 * g + 4, 0:30], in_=ptl[s][:, c, 0:120].rearrange("p (n r) -> p n r", n=4), func=AF.Copy)
                    cs_S = kb.dsem()
                    kb.dma("sp", cs_S, convs_d[:, 0:22, :], sconv_d[:, 8:30, :], [], [])
                kb.soft_barrier()
                RC.p = mark
                with contextlib.ExitStack() as s2:
                    pv = [ps(s2, "cpv%d" % i, (P, 512), F32) for i in range(2)]
                    pg = [ps(s2, "cpg%d" % i, (P, 512), F32) for i in range(2)]
                    pvB, pgB = [Buf(), Buf()], [Buf(), Buf()]
                    sg = [RC.alloc((P, 416), F32) for i in range(2)]
                    sgB = [Buf(), Buf()]
                    cblocks = [(0, 384), (384, 384), (768, 416)]
                    cnt = 0
                    for cp in range(4):
                        wv, wvB = wload_sm(w_in_d[:, cp * 256:(cp + 1) * 256], KC, 256, key="cv%d" % cp)
                        wg, wgB = wload_sm(w_in_d[:, 1024 + cp * 256:1024 + (cp + 1) * 256], KC, 256, key="cg%d" % cp)
                        for ci in range(2):
                            c = cp * 2 + ci
                            for (c0, n) in cblocks:
                                s = cnt % 2
                                cnt += 1
                                kb.mm_group(pv[s][:, 0:n], [(wv[:, k, ci * P:(ci + 1) * P], hT[:, k, c0:c0 + n]) for k in range(KC)], [hTB] + wvB, [pvB[s]])
                                kb.mm_group(pg[s][:, 0:n], [(wg[:, k, ci * P:(ci + 1) * P], hT[:, k, c0:c0 + n]) for k in range(KC)], [hTB] + wgB, [pgB[s]])
                                kb.op("act", "activation", [pgB[s]], [sgB[s]], out=sg[s][:, 0:n], in_=pg[s][:, 0:n], func=AF.Sigmoid)
                                if c0 < 768:
                                    kb.op("dve", "tensor_tensor", [pvB[s], sgB[s]], [upB], out=u_p[:, c, c0:c0 + n], in0=pv[s][:, 0:n], in1=sg[s][:, 0:n], op=ALU.mult)
                                else:
                                    kb.op("dve", "tensor_tensor", [pvB[s], sgB[s]], [upB], out=u_p[:, c, 768:1056], in0=pv[s][:, 0:288], in1=sg[s][:, 0:288], op=ALU.mult)
                                    kb.op("dve", "tensor_tensor", [pvB[s], sgB[s]], [uftB], out=ufp_t[:, c, :], in0=pv[s][:, 256:288], in1=sg[s][:, 256:288], op=ALU.mult)
                                    kb.op("dve", "tensor_tensor", [pvB[s], sgB[s]], [ufsB], out=ufp_s[:, c, :], in0=pv[s][:, 288:416], in1=sg[s][:, 288:416], op=ALU.mult)
                                    kb.op("act", "activation", [ufsB], [usB], out=u_s[:, c, :, 30:38], in_=ufp_s[:, c, :].rearrange("p (n t) -> p n t", t=LS), func=AF.Copy)
                kb.soft_barrier()
                RC.p = mark
                with contextlib.ExitStack() as s2:
                    pts = ps(s2, "cpts", (P, 8, P), F32)
                    ptt = ps(s2, "cptt", (32, 8, P), F32)
                    ptsB, pttB = Buf(), Buf()
                    us_tm = RC.alloc((P, 1024), F32)
                    ut_tm = RC.alloc((32, 1024), F32)
                    ustB, uttB = Buf(), Buf()
                    kb.transposes([(pts[:, c, :], ufp_s[:, c, :], ident_f) for c in range(8)], [ufsB, constsB], [ptsB])
                    kb.op("act", "activation", [ptsB], [ustB], out=us_tm, in_=pts[:].rearrange("p a b -> p (a b)"), func=AF.Copy)
                    kb.transposes([(ptt[:, c, :], ufp_t[:, c, :], ident_f) for c in range(8)], [uftB, constsB], [pttB])
                    kb.op("act", "activation", [pttB], [uttB], out=ut_tm, in_=ptt[:].rearrange("p a b -> p (a b)"), func=AF.Copy)
                    for n_ in range(NSEQ):
                        kb.dma("sp", cs_S, convs_d[n_, 22:30, :], us_tm[n_ * LS:(n_ + 1) * LS, :], [ustB], [])
                    kb.dma("sp", cs_S, convp_d[:, :], ut_tm, [uttB], [])
                kb.soft_barrier()
                RC.p = mark
                with contextlib.ExitStack() as s2:
                    dg = [RC.alloc((P, 31, P), BF16) for i in range(2)]
                    dgB = [Buf(), Buf()]
                    pc = [ps(s2, "cpc%d" % i, (P, 512), F32) for i in range(3)]
                    pcB = [Buf(), Buf(), Buf()]
                    pst = [ps(s2, "cpst%d" % i, (P, 512), F32) for i in range(2)]
                    pstB = [Buf(), Buf()]
                    sqc = [RC.alloc((P, 512), BF16) for i in range(2)]
                    sqcB = [Buf(), Buf()]
                    tbs = [(0, 512), (512, 512), (1024, 128)]
                    cnt = 0
                    def build_dg(c):
                        d = c % 2
                        wcol = cols[:, C_CONVW + c:C_CONVW + c + 31 * 8 - 7:8]
                        kb.op("dve", "tensor_tensor", [identB, colsB], [dgB[d]], out=dg[d], in0=ident_bf.unsqueeze(1).to_broadcast([P, 31, P]), in1=wcol.unsqueeze(2).to_broadcast([P, 31, P]), op=ALU.mult)
                    build_dg(0)
                    for c in range(8):
                        d = c % 2
                        if c + 1 < 8:
                            build_dg(c + 1)
                        for ti, (t0, n) in enumerate(tbs):
                            s = cnt % 3
                            s2_ = cnt % 2
                            cnt += 1
                            if ti < 2:
                                pairs = [(dg[d][:, j, :], u_p[:, c, t0 + 2 + j:t0 + 2 + j + n]) for j in range(31)]
                                outap = pc[s][:, 0:n]
                            else:
                                pairs = [(dg[d][:, j, :], u_s[:, c, :, j:j + LS]) for j in range(31)]
                                outap = pc[s][:, 0:n].rearrange("p (a b) -> p a b", b=LS)
                            kb.mm_group(outap, pairs, [dgB[d], upB, usB], [pcB[s]])
                            kb.op("act", "activation", [pcB[s], colsB], [dcbB], out=dcb[:, c, t0:t0 + n], in_=pc[s][:, 0:n], func=AF.Identity, bias=cols[:, C_CONVB + c:C_CONVB + c + 1])
                            kb.op("act", "activation", [dcbB], [sqcB[s2_]], out=sqc[s2_][:, 0:n], in_=dcb[:, c, t0:t0 + n], func=AF.Square)
                            kb.mm_group(pst[0][:, 0:n], [(ones_bf, dcb[:, c, t0:t0 + n])], [onesB, dcbB], [pstB[0]])
                            kb.mm_group(pst[1][:, 0:n], [(ones_bf, sqc[s2_][:, 0:n])], [onesB, sqcB[s2_]], [pstB[1]])
                            if c == 0:
                                kb.op("dve", "tensor_copy", [pstB[0]], [S1B], out=S1[:, t0:t0 + n], in_=pst[0][:, 0:n])
                                kb.op("dve", "tensor_copy", [pstB[1]], [S2B], out=S2[:, t0:t0 + n], in_=pst[1][:, 0:n])
                            else:
                                kb.op("dve", "tensor_tensor", [pstB[0], S1B], [S1B], out=S1[:, t0:t0 + n], in0=pst[0][:, 0:n], in1=S1[:, t0:t0 + n], op=ALU.add)
                                kb.op("dve", "tensor_tensor", [pstB[1], S2B], [S2B], out=S2[:, t0:t0 + n], in0=pst[1][:, 0:n], in1=S2[:, t0:t0 + n], op=ALU.add)
                kb.soft_barrier()
                RC.p = mark
                with contextlib.ExitStack() as s2:
                    msq = RC.alloc((P, T), F32)
                    msqB = Buf()
                    t1 = [RC.alloc((P, 384), F32) for i in range(2)]
                    t1B = [Buf(), Buf()]
                    kb.op("dve", "tensor_scalar", [S1B], [S1B], out=S1, in0=S1, scalar1=1.0 / 1024, scalar2=None, op0=ALU.mult)
                    kb.op("dve", "tensor_tensor", [S1B], [msqB], out=msq, in0=S1, in1=S1, op=ALU.mult)
                    kb.op("dve", "scalar_tensor_tensor", [S2B, msqB], [S2B], out=S2, in0=S2, scalar=1.0 / 1024, in1=msq, op0=ALU.mult, op1=ALU.subtract)
                    kb.op("act", "activation", [S2B], [S2B], out=S2, in_=S2, func=AF.Ln, bias=EPS)
                    kb.op("act", "activation", [S2B], [S2B], out=S2, in_=S2, func=AF.Exp, scale=-0.5)
                    pg = [ps(s2, "csg%d" % i, (P, 512), F32) for i in range(2)]
                    pgB = [Buf(), Buf()]
                    sl = [RC.alloc((P, 384), BF16) for i in range(2)]
                    slB = [Buf(), Buf()]
                    cnt = 0
                    for cb in range(2):
                        wg, wgB = wload_big(w_in_d[:, 2048 + cb * 512:2048 + (cb + 1) * 512], KC, 512)
                        if cb == 1 and (ALL or "attn" in phases):
                            wcache["kv0"] = wload_big(w_kv_d[:, 0:512], KC, 512)
                        for ci in range(4):
                            c = cb * 4 + ci
                            for tb in range(3):
                                s = cnt % 2
                                cnt += 1
                                c0 = tb * 384
                                kb.mm_group(pg[s][:, 0:384], [(wg[:, k, ci * P:(ci + 1) * P], hT[:, k, TAIL + c0:TAIL + c0 + 384]) for k in range(KC)], [hTB] + wgB, [pgB[s]])
                                kb.op("dve", "tensor_tensor", [dcbB, S1B], [t1B[s]], out=t1[s], in0=dcb[:, c, c0:c0 + 384], in1=S1[:, c0:c0 + 384], op=ALU.subtract)
                                kb.op("dve", "tensor_tensor", [t1B[s], S2B], [t1B[s]], out=t1[s], in0=t1[s], in1=S2[:, c0:c0 + 384], op=ALU.mult)
                                kb.op("act", "activation", [t1B[s], colsB], [aB], out=a_[:, c, c0:c0 + 384], in_=t1[s], func=AF.Silu, scale=cols[:, C_LNG + c:C_LNG + c + 1], bias=cols[:, C_LNB + c:C_LNB + c + 1])
                                kb.op("act", "activation", [pgB[s]], [slB[s]], out=sl[s], in_=pg[s][:, 0:384], func=AF.Silu)
                                kb.op("dve", "tensor_tensor", [aB, slB[s]], [aB], out=a_[:, c, c0:c0 + 384], in0=a_[:, c, c0:c0 + 384], in1=sl[s], op=ALU.mult)
                kb.barrier()
            if "dbg" in phases:
                kb.dma("sp", kb.dsem(), dbg["a"], a_, [aB], [])

        alvl = 9
        for p_ in phases:
            if p_.startswith("attn="):
                alvl = int(p_.split("=")[1])
        if ALL or "attn" in phases or alvl < 9:
            RA = Region(149 * KIB, 200 * KIB)
            RS2 = Region(14 * KIB, 26 * KIB)
            with contextlib.ExitStack() as st:
                memT = RS2.alloc((P, KC, 256), BF16)
                memTB = Buf()
                mark0 = RA.p
                with contextlib.ExitStack() as s1:
                  if alvl >= 1:
                    jobs = [(mem_d[i * P:(i + 1) * P, :], C_GMEM, memT[:, :, i * P:(i + 1) * P], memTB) for i in range(2)]
                    norm_phase(RA, s1, jobs, "na")
                kb.soft_barrier()
                RA.p = mark0
                qTa = RA.alloc((P, 8, T), BF16)
                mark = RA.p
                KTp = RA.alloc((P, 8, 256), BF16)
                Vp = RA.alloc((P, 2, 1024), BF16)
                qTaB, KTpB, VpB = Buf(), Buf(), Buf()
                mark2 = RA.p
                with contextlib.ExitStack() as s2:
                  if alvl >= 2:
                    pk = [ps(s2, "apk%d" % i, (P, 512), F32) for i in range(2)]
                    pkB = [Buf(), Buf()]
                    pkt = [ps(s2, "apkt%d" % i, (P, 256), F32) for i in range(2)]
                    pktB = [Buf(), Buf()]
                    stg = [RA.alloc((P, 512), F32) for i in range(2)]
                    stgB = [Buf(), Buf()]
                    stgS = [kb.dsem(), kb.dsem()]
                    cnt = 0
                    cnt2 = 0
                    nblk = 4
                    for p_ in phases:
                        if p_.startswith("blk="):
                            nblk = int(p_.split("=")[1])
                    for blk in range(nblk):
                        wk, wkB = wload_big(w_kv_d[:, blk * 512:(blk + 1) * 512], KC, 512, key="kv%d" % blk)
                        isk = blk < 2
                        dst_d = mk_d if isk else mv_d
                        cb = blk % 2
                        for mt in range(2):
                            s = cnt % 2
                            cnt += 1
                            if "nomm" not in phases:
                                kb.mm_group(pk[s][:], [(memT[:, k, mt * P:(mt + 1) * P], wk[:, k, :]) for k in range(KC)], [memTB] + wkB, [pkB[s]])
                            if "noact" not in phases:
                                kb.op("act", "activation", [pkB[s]], [stgB[s]], out=stg[s], in_=pk[s][:], func=AF.Copy)
                                if "skipst" not in phases:
                                    kb.dma("sp", stgS[s], dst_d[mt * P:(mt + 1) * P, cb * 512:(cb + 1) * 512], stg[s], [stgB[s]], [])
                            if not isk and "novp" not in phases:
                                kb.op("dve", "tensor_copy", [stgB[s]], [VpB], out=Vp[:, mt, cb * 512:(cb + 1) * 512], in_=stg[s])
                        if isk and "skipkt" not in phases:
                            for dc in range(4):
                                s = cnt2 % 2
                                cnt2 += 1
                                kb.mm_group(pkt[s][:], [(wk[:, k, dc * P:(dc + 1) * P], memT[:, k, :]) for k in range(KC)], [memTB] + wkB, [pktB[s]])
                                kb.op("dve", "tensor_copy", [pktB[s]], [KTpB], out=KTp[:, cb * 4 + dc, :], in_=pkt[s][:])
                kb.soft_barrier()
                RA.p = mark2
                with contextlib.ExitStack() as s2:
                  if alvl >= 3:
                    pq = [ps(s2, "apq%d" % i, (P, 512), F32) for i in range(2)]
                    pqB = [Buf(), Buf()]
                    cnt = 0
                    for cb in range(2):
                        wq, wqB = wload_big(w_in_d[:, 7168 + cb * 512:7168 + (cb + 1) * 512], KC, 512)
                        for ci in range(4):
                            c = cb * 4 + ci
                            for tb in range(3):
                                s = cnt % 2
                                cnt += 1
                                c0 = tb * 384
                                kb.mm_group(pq[s][:, 0:384], [(wq[:, k, ci * P:(ci + 1) * P], hT[:, k, TAIL + c0:TAIL + c0 + 384]) for k in range(KC)], [hTB] + wqB, [pqB[s]])
                                kb.op("act", "activation", [pqB[s]], [qTaB], out=qTa[:, c, c0:c0 + 384], in_=pq[s][:, 0:384], func=AF.Copy)
                kb.soft_barrier()
                SC = float(256) ** -0.5
                with contextlib.ExitStack() as s2:
                  if alvl >= 4:
                    sc = [ps(s2, "asc%d" % i, (P, 512), F32) for i in range(2)]
                    scB = [Buf(), Buf()]
                    pden = ps(s2, "aden", (P, 512), F32)
                    pdenB = Buf()
                    pnum = [ps(s2, "anum%d" % i, (P, 512), F32) for i in range(2)]
                    pnumB = [Buf(), Buf()]
                    ET = [RA.alloc((P, 512), BF16) for i in range(2)]
                    ETB = [Buf(), Buf()]
                    rden = RA.alloc((P, 512), F32)
                    rdenB = Buf()
                    for tb in range(2):
                        for h in range(4):
                            for mt in range(2):
                                kb.mm_group(sc[mt][:], [(KTp[:, 2 * h + dc, mt * P:(mt + 1) * P], qTa[:, 2 * h + dc, tb * 512:(tb + 1) * 512]) for dc in range(2)], [KTpB, qTaB], [scB[mt]])
                                kb.op("act", "activation", [scB[mt]], [ETB[mt]], out=ET[mt], in_=sc[mt][:], func=AF.Exp, scale=SC)
                            kb.mm_group(pden[:], [(ones_bf, ET[0]), (ones_bf, ET[1])], [onesB, ETB[0], ETB[1]], [pdenB])
                            kb.op("dve", "reciprocal", [pdenB], [rdenB], out=rden, in_=pden[:])
                            for dc in range(2):
                                kb.mm_group(pnum[dc][:], [(Vp[:, mt, (2 * h + dc) * P:(2 * h + dc + 1) * P], ET[mt]) for mt in range(2)], [VpB, ETB[0], ETB[1]], [pnumB[dc]])
                                kb.op("dve", "tensor_tensor", [pnumB[dc], rdenB], [aoB], out=ao[:, 2 * h + dc, tb * 512:(tb + 1) * 512], in0=pnum[dc][:], in1=rden, op=ALU.mult)
                kb.soft_barrier()
                RA.p = mark
                with contextlib.ExitStack() as s2:
                  if alvl >= 5:
                    Kn = [RA.alloc((P, 2, 1024), BF16) for i in range(2)]
                    Vn = [RA.alloc((P, 2, 1024), BF16) for i in range(2)]
                    KTn = [RA.alloc((P, 8, 256), BF16) for i in range(2)]
                    KnB, VnB, KTnB = [Buf(), Buf()], [Buf(), Buf()], [Buf(), Buf()]
                    KnS, VnS = [kb.dsem(), kb.dsem()], [kb.dsem(), kb.dsem()]
                    ETs = RA.alloc((P, 2, NSEQ, 4, LS), BF16)
                    ETsB = Buf()
                    rds = RA.alloc((P, NSEQ, 4, LS), F32)
                    rdsB = Buf()
                    pkt2 = [ps(s2, "skt%d" % i, (P, 8, 256), BF16) for i in range(1)]
                    pkt2B = [Buf()]
                    scs = [ps(s2, "sscs%d" % i, (P, 512), F32) for i in range(2)]
                    scsB = [Buf(), Buf()]
                    nums = ps(s2, "snum", (P, 8, P), F32)
                    numsB = Buf()
                    dens = ps(s2, "sden", (P, 512), F32)
                    densB = Buf()
                    for n_ in range(NSEQ):
                        s = n_ % 2
                        kb.dma("pool", KnS[s], Kn[s], kc_d[n_].rearrange("(mt p) f -> p mt f", p=P), [], [KnB[s]])
                        kb.dma("pool", VnS[s], Vn[s], vc_d[n_].rearrange("(mt p) f -> p mt f", p=P), [], [VnB[s]])
                        kb.transposes([(pkt2[0][:, ch, mt * P:(mt + 1) * P], Kn[s][:, mt, ch * P:(ch + 1) * P], ident_bf) for ch in range(8) for mt in range(2)], [KnB[s], identB], [pkt2B[0]])
                        kb.op("act", "activation", [pkt2B[0]], [KTnB[s]], out=KTn[s][:, 0:4, :], in_=pkt2[0][:, 0:4, :], func=AF.Copy)
                        kb.op("dve", "tensor_copy", [pkt2B[0]], [KTnB[s]], out=KTn[s][:, 4:8, :], in_=pkt2[0][:, 4:8, :])
                        for h in range(4):
                            for mt in range(2):
                                kb.mm_group(scs[s][:, (mt * 4 + h) * LS:(mt * 4 + h + 1) * LS], [(KTn[s][:, 2 * h + dc, mt * P:(mt + 1) * P], qTa[:, 2 * h + dc, TP + n_ * LS:TP + (n_ + 1) * LS]) for dc in range(2)], [KTnB[s], qTaB], [scsB[s]])
                        kb.op("act", "activation", [scsB[s]], [ETsB], out=ETs[:, :, n_, :, :].rearrange("p m h t -> p m (h t)"), in_=scs[s][:, 0:64].rearrange("p (m x) -> p m x", m=2), func=AF.Exp, scale=SC)
                        for h in range(4):
                            for dc in range(2):
                                kb.mm_group(nums[:, 2 * h + dc, n_ * LS:(n_ + 1) * LS], [(Vn[s][:, mt, (2 * h + dc) * P:(2 * h + dc + 1) * P], ETs[:, mt, n_, h, :]) for mt in range(2)], [VnB[s], ETsB], [numsB])
                    kb.mm_group(dens[:], [(ones_bf, ETs[:, mt].rearrange("p n h t -> p (n h t)")) for mt in range(2)], [onesB, ETsB], [densB])
                    kb.op("dve", "reciprocal", [densB], [rdsB], out=rds.rearrange("p n h t -> p (n h t)"), in_=dens[:])
                    for h in range(4):
                        for dc in range(2):
                            kb.op("dve", "tensor_tensor", [numsB, rdsB], [aoB], out=ao[:, 2 * h + dc, TP:T].rearrange("p (n t) -> p n t", t=LS),
                                  in0=nums[:, 2 * h + dc, :].rearrange("p (n t) -> p n t", t=LS), in1=rds[:, :, h, :], op=ALU.mult)
                kb.soft_barrier()
                RA.p = mark0
                with contextlib.ExitStack() as s2:
                  if alvl >= 6:
                    pg = [ps(s2, "asg%d" % i, (P, 512), F32) for i in range(2)]
                    pgB = [Buf(), Buf()]
                    sl = [RA.alloc((P, 384), BF16) for i in range(2)]
                    slB = [Buf(), Buf()]
                    cnt = 0
                    for cb in range(2):
                        wg, wgB = wload_big(w_in_d[:, 8192 + cb * 512:8192 + (cb + 1) * 512], KC, 512)
                        for ci in range(4):
                            c = cb * 4 + ci
                            for tb in range(3):
                                s = cnt % 2
                                cnt += 1
                                c0 = tb * 384
                                kb.mm_group(pg[s][:, 0:384], [(wg[:, k, ci * P:(ci + 1) * P], hT[:, k, TAIL + c0:TAIL + c0 + 384]) for k in range(KC)], [hTB] + wgB, [pgB[s]])
                                kb.op("act", "activation", [pgB[s]], [slB[s]], out=sl[s], in_=pg[s][:, 0:384], func=AF.Silu)
                                kb.op("dve", "tensor_tensor", [aoB, slB[s]], [aoB], out=ao[:, c, c0:c0 + 384], in0=ao[:, c, c0:c0 + 384], in1=sl[s], op=ALU.mult)
                kb.soft_barrier()
            if "dbg" in phases:
                kb.dma("sp", kb.dsem(), dbg["ao"], ao, [aoB], [])

        if ALL or "merge" in phases:
            RM = Region(149 * KIB, 200 * KIB)
            RS2 = Region(14 * KIB, 26 * KIB)
            merged = RM.alloc((P, KC, T), BF16)
            mgB = Buf()
            with contextlib.ExitStack() as s2:
                acc = RM.alloc((P, 2, T), F32)
                accB = Buf()
                gt = [RS2.alloc((P, 384), F32) for i in range(2)]
                gtB = [Buf(), Buf()]
                tt = [RS2.alloc((P, 384), F32) for i in range(2)]
                ttB = [Buf(), Buf()]
                pgt = [ps(s2, "mpg%d" % i, (P, 512), F32) for i in range(3)]
                pgtB = [Buf() for _ in range(3)]
                ppp = [ps(s2, "mpp%d" % i, (P, 512), F32) for i in range(3)]
                pppB = [Buf() for _ in range(3)]
                srcs = [(a_, aB, w_co_d), (ob, obB, w_ho_d), (ao, aoB, w_ao_d)]
                cnt = 0
                for g in range(8):
                    for i in range(3):
                        bsrc, bsrcB, wdr = srcs[i]
                        wg, wgB = wload_sm(w_in_d[:, 9216 + i * 2048 + g * 256:9216 + i * 2048 + (g + 1) * 256], KC, 256)
                        wo, woB = wload_sm(wdr[:, g * 256:(g + 1) * 256], 8, 256)
                        for fi in range(2):
                            f = g * 2 + fi
                            for tb in range(3):
                                s = cnt % 3
                                s2_ = cnt % 2
                                cnt += 1
                                c0 = tb * 384
                                kb.mm_group(pgt[s][:, 0:384], [(wg[:, k, fi * P:(fi + 1) * P], hT[:, k, TAIL + c0:TAIL + c0 + 384]) for k in range(KC)], [hTB] + wgB, [pgtB[s]])
                                kb.mm_group(ppp[s][:, 0:384], [(wo[:, k, fi * P:(fi + 1) * P], bsrc[:, k, c0:c0 + 384]) for k in range(8)], [bsrcB] + woB, [pppB[s]])
                                kb.op("act", "activation", [pgtB[s], colsB], [gtB[s2_]], out=gt[s2_], in_=pgt[s][:, 0:384], func=AF.Sigmoid, bias=cols[:, C_BGATE + i * 16 + f:C_BGATE + i * 16 + f + 1])
                                if i == 0:
                                    kb.op("dve", "tensor_tensor", [pppB[s], gtB[s2_]], [accB], out=acc[:, fi, c0:c0 + 384], in0=ppp[s][:, 0:384], in1=gt[s2_], op=ALU.mult)
                                else:
                                    kb.op("dve", "tensor_tensor", [pppB[s], gtB[s2_]], [ttB[s2_]], out=tt[s2_], in0=ppp[s][:, 0:384], in1=gt[s2_], op=ALU.mult)
                                    if i == 1:
                                        kb.op("dve", "tensor_tensor", [ttB[s2_], accB], [accB], out=acc[:, fi, c0:c0 + 384], in0=acc[:, fi, c0:c0 + 384], in1=tt[s2_], op=ALU.add)
                                    else:
                                        kb.op("dve", "tensor_tensor", [ttB[s2_], accB], [mgB], out=merged[:, f, c0:c0 + 384], in0=acc[:, fi, c0:c0 + 384], in1=tt[s2_], op=ALU.add)
            kb.soft_barrier()
            if "dbg" in phases:
                kb.dma("sp", kb.dsem(), dbg["mg"], merged, [mgB], [])
            RO = Region(58 * KIB, 149 * KIB)
            with contextlib.ExitStack() as s2:
                xres = RO.alloc((P, 9, D), F32)
                gfin = RO.alloc((P, D), F32)
                xrB = [Buf() for _ in range(9)]
                xrS = [kb.dsem() for _ in range(9)]
                gfB = Buf()
                kb.dma("sp", kb.dsem(), gfin, gfin_d[:, :], [], [gfB])
                for tl in range(9):
                    kb.dma("sp", xrS[tl], xres[:, tl, :], x_d[tl * P:(tl + 1) * P, :], [], [xrB[tl]])
                py = [ps(s2, "ypy%d" % i, (P, 512), F32) for i in range(4)]
                pyB = [Buf() for _ in range(4)]
                junk2 = RS2.alloc((P, D), BF16)
                j2B = Buf()
                def final_norm(tl):
                        sB = Buf()
                        kb.op("act", "activation", [xrB[tl]], [j2B, sB], out=junk2, in_=xres[:, tl, :], func=AF.Square, accum_out=gstat[:, 3 * tl:3 * tl + 1])
                        kb.op("act", "activation", [sB], [sB], out=gstat[:, 3 * tl + 1:3 * tl + 2], in_=gstat[:, 3 * tl:3 * tl + 1], func=AF.Ln, scale=1.0 / D, bias=EPS)
                        kb.op("act", "activation", [sB], [sB], out=gstat[:, 3 * tl + 2:3 * tl + 3], in_=gstat[:, 3 * tl + 1:3 * tl + 2], func=AF.Exp, scale=-0.5)
                        kb.op("act", "activation", [xrB[tl], sB], [xrB[tl]], out=xres[:, tl, :], in_=xres[:, tl, :], func=AF.Copy, scale=gstat[:, 3 * tl + 2:3 * tl + 3])
                        kb.op("pool", "tensor_tensor", [xrB[tl], gfB], [xrB[tl]], out=xres[:, tl, :], in0=xres[:, tl, :], in1=gfin, op=ALU.mult)
                        kb.dma("sp", xrS[tl], y_d[tl * P:(tl + 1) * P, :], xres[:, tl, :], [xrB[tl]], [])


                cnt = 0
                for cb in range(4):
                    wo, woB = wload_big(w_out_d[:, cb * 512:(cb + 1) * 512], KC, 512)
                    for tl in range(9):
                        s = cnt % 4
                        cnt += 1
                        kb.mm_group(py[s][:], [(merged[:, k, tl * P:(tl + 1) * P], wo[:, k, :]) for k in range(KC)], [mgB] + woB, [pyB[s]])
                        kb.op("dve", "tensor_tensor", [pyB[s], xrB[tl]], [xrB[tl]], out=xres[:, tl, cb * 512:(cb + 1) * 512], in0=py[s][:], in1=xres[:, tl, cb * 512:(cb + 1) * 512], op=ALU.add)
                        if cb == 3:
                            final_norm(tl)
        assert not wcache, list(wcache)
        kb.barrier()
        with nc.Block() as block:
            @block.tensor
            def _(e):
                kb.replay(e, "pe")

            @block.scalar
            def _(e):
                kb.replay(e, "act")

            @block.vector
            def _(e):
                kb.replay(e, "dve")

            @block.gpsimd
            def _(e):
                kb.replay(e, "pool")

            @block.sync
            def _(e):
                kb.replay(e, "sp")
    return nc


def make_consts():
    c = np.zeros((P, NCONST), np.float32)
    c[:, K_IDENT:K_IDENT + P] = np.eye(P, dtype=np.float32)
    s = np.arange(P)[:, None]
    t = np.arange(P)[None, :]
    c[:, K_MPR:K_MPR + P] = ((s // 64 == t // 64) & (s <= t)).astype(np.float32)
    c[:, K_MSM:K_MSM + P] = ((s // LS == t // LS) & (s <= t)).astype(np.float32)
    c[:, K_SEL:K_SEL + NSEQ] = (s // LS == np.arange(NSEQ)[None, :]).astype(np.float32)
    r = np.ones(T, np.float32)
    r[0:TP:64] = 0.0
    r[TP::LS] = 0.0
    c[:, K_RST:K_RST + T] = r[None, :]
    return c


def colmaj(v):
    v = np.asarray(v, np.float32)
    return np.ascontiguousarray(v.reshape(-1, P).T)


def make_cols(inp):
    c = np.zeros((P, NCOLS), np.float32)
    c[:, C_GPRE:C_GPRE + 16] = colmaj(inp["g_pre"][0])
    c[:, C_GMEM:C_GMEM + 16] = colmaj(inp["g_mem"][0])
    c[:, C_CONVB:C_CONVB + 8] = colmaj(inp["conv_b"][0])
    c[:, C_LNG:C_LNG + 8] = colmaj(inp["ln_conv_g"][0])
    c[:, C_LNB:C_LNB + 8] = colmaj(inp["ln_conv_b"][0])
    c[:, C_LB0:C_LB0 + 8] = colmaj(inp["lb_logits"][0])
    c[:, C_LB1:C_LB1 + 8] = colmaj(inp["lb_logits"][1])
    c[:, C_GHG:C_GHG + 8] = colmaj(inp["g_hgrn_norm"][0])
    for i in range(3):
        c[:, C_BGATE + 16 * i:C_BGATE + 16 * (i + 1)] = colmaj(inp["b_gate"][0, i])
    for j in range(31):
        c[:, C_CONVW + 8 * j:C_CONVW + 8 * (j + 1)] = colmaj(inp["conv_w"][0, j])
    return c


def make_in_maps(inp):
    consts = make_consts()
    cols = make_cols(inp)
    gfin = np.ascontiguousarray(np.broadcast_to(np.asarray(inp["g_final"], np.float32)[None, :], (P, D)))
    xp = np.asarray(inp["x_prompt"], np.float32)
    xs = np.asarray(inp["x_sample"], np.float32)
    maps = []
    for c in range(NCORES):
        b, hf = c // 2, c % 2
        x = np.concatenate([xp[b, hf * TP:(hf + 1) * TP], xs[c * NSEQ:(c + 1) * NSEQ].reshape(TS, D)], axis=0)
        xprev = xp[b, 0:TP] if hf == 1 else np.zeros((TP, D), np.float32)
        maps.append({
            "x": np.ascontiguousarray(x),
            "xprev": np.ascontiguousarray(xprev),
            "mem": np.ascontiguousarray(inp["mem_prompt"][b]),
            "kc": np.ascontiguousarray(np.asarray(inp["cache_mem_k"])[0, c * NSEQ:(c + 1) * NSEQ].reshape(NSEQ, 256, 1024)),
            "vc": np.ascontiguousarray(np.asarray(inp["cache_mem_v"])[0, c * NSEQ:(c + 1) * NSEQ].reshape(NSEQ, 256, 1024)),
            "sconv": np.ascontiguousarray(np.asarray(inp["state_conv"])[0, c * NSEQ:(c + 1) * NSEQ]),
            "shgrn": np.ascontiguousarray(np.asarray(inp["state_hgrn"])[0, c * NSEQ:(c + 1) * NSEQ]),
            "w_in": np.ascontiguousarray(inp["w_in"][0]),
            "w_conv_out": np.ascontiguousarray(inp["w_conv_out"][0]),
            "w_hgrn_out": np.ascontiguousarray(inp["w_hgrn_out"][0]),
            "w_attn_out": np.ascontiguousarray(inp["w_attn_out"][0]),
            "w_mem_kv": np.ascontiguousarray(inp["w_mem_kv"][0]),
            "w_out": np.ascontiguousarray(inp["w_out"][0]),
            "cols": cols,
            "consts": consts,
            "gfin": gfin,
        })
    return maps


def assemble(res):
    y_p = np.zeros((4, 2048, D), np.float32)
    y_s = np.zeros((128, 8, D), np.float32)
    conv_p = np.zeros((1, 4, 30, 1024), np.float32)
    hg_p = np.zeros((1, 4, 8, P, P), np.float32)
    mk = np.zeros((1, 4, 256, 4, 256), np.float32)
    mv = np.zeros((1, 4, 256, 4, 256), np.float32)
    conv_s = np.zeros((1, 128, 30, 1024), np.float32)
    hg_s = np.zeros((1, 128, 8, P, P), np.float32)
    for c in range(NCORES):
        r = res[c]
        b, hf = c // 2, c % 2
        y_p[b, hf * TP:(hf + 1) * TP] = r["y"][0:TP]
        y_s[c * NSEQ:(c + 1) * NSEQ] = r["y"][TP:].reshape(NSEQ, LS, D)
        conv_s[0, c * NSEQ:(c + 1) * NSEQ] = r["conv_s"]
        hg_s[0, c * NSEQ:(c + 1) * NSEQ] = r["hgrn_s"]
        if hf == 1:
            conv_p[0, b] = r["conv_p"][2:32]
            hg_p[0, b] = r["hgrn_p"]
        else:
            mk[0, b] = r["mk"].reshape(256, 4, 256)
            mv[0, b] = r["mv"].reshape(256, 4, 256)
    return (y_p, y_s, conv_p, hg_p, mk, mv, conv_s, hg_s)


_CACHE = {}


def kernel(**inputs):
    inp = {k: np.asarray(v) for k, v in inputs.items()}
    if "nc" not in _CACHE:
        _CACHE["nc"] = build_program()
    nc = _CACHE["nc"]
    maps = make_in_maps(inp)
    res = run_bass_kernel_spmd(nc, maps, core_ids=list(range(NCORES)))
    return assemble(res.results)
```
